# Optimizing a Trainium2 kernel written in Bass

```python
import functools
import jax, jax.numpy as jnp
from jax import lax
import numpy as np

D_MODEL = 4096
BATCH = 8
SEQ = 2048
DEPTH = 2
DEC_BATCH = 16
DEC_SEQ = 32
PAST_LEN = 2048

CHUNK = 64
LEFT_CHUNKS = 8
BAND_CHUNKS = LEFT_CHUNKS + 1
ATT_PAST = LEFT_CHUNKS * CHUNK
WIDTH_ATT = D_MODEL // 2
WIDTH_SC = D_MODEL // 4
WIDTH_CV = D_MODEL - WIDTH_ATT - WIDTH_SC
HEAD_DIM = 128
N_HEADS_ATT = WIDTH_ATT // HEAD_DIM
MAX_REL = 256
N_REL = 2 * MAX_REL + 1
SC_WIDTH = 3
CV_WIDTH = 31
D_FF = -(-8 * D_MODEL // (3 * 256)) * 256
SPLIT_SIZES = [WIDTH_ATT] * 3 + [WIDTH_SC] * 3 + [WIDTH_CV] * 2
D_IN = sum(SPLIT_SIZES)
SPLITS = [int(s) for s in np.cumsum(SPLIT_SIZES)[:-1]]
EPS = 1e-6

kernel_name = "hybrid_chunk_stream_encoder_step"


def rmsnorm(x, g):
    xf = x.astype(jnp.float32)
    y = xf * lax.rsqrt(jnp.mean(xf * xf, axis=-1, keepdims=True) + EPS)
    return (y * g.astype(jnp.float32)).astype(x.dtype)


def layernorm(x, g, b):
    xf = x.astype(jnp.float32)
    mu = jnp.mean(xf, axis=-1, keepdims=True)
    var = jnp.mean(jnp.square(xf - mu), axis=-1, keepdims=True)
    y = (xf - mu) * lax.rsqrt(var + EPS)
    return (y * g.astype(jnp.float32) + b.astype(jnp.float32)).astype(x.dtype)


def causal_dwconv(u, hist, w):
    full = jnp.concatenate([hist.astype(u.dtype), u], axis=1)
    y = lax.conv_general_dilated(full, w[:, None, :].astype(u.dtype), (1,), 'VALID',
                                 dimension_numbers=('NWC', 'WIO', 'NWC'),
                                 feature_group_count=u.shape[-1])
    return y, full[:, full.shape[1] - (w.shape[0] - 1):]


def band_attention(q, k, v, n_past, rel_bias, key_valid):
    tq, tk = q.shape[2], k.shape[2]
    dist = jnp.arange(tq)[:, None] + n_past - jnp.arange(tk)[None, :]
    bias = rel_bias[:, jnp.clip(dist, -MAX_REL, MAX_REL) + MAX_REL].astype(jnp.float32)
    s = jnp.einsum('bnqhd,bnkhd->bnhqk', q, k, preferred_element_type=jnp.float32) * (HEAD_DIM ** -0.5) + bias
    if key_valid is not None:
        s = jnp.where(key_valid[None, :, None, None, :], s, jnp.finfo(jnp.float32).min)
    p = jax.nn.softmax(s, axis=-1).astype(v.dtype)
    return jnp.einsum('bnhqk,bnkhd->bnqhd', p, v)


def prompt_attention(q, k, v, rel_bias):
    b, s = q.shape[:2]
    nc = s // CHUNK
    chunk = lambda t: t.reshape(b, nc, CHUNK, N_HEADS_ATT, HEAD_DIM)
    pad = ((0, 0), (LEFT_CHUNKS, 0), (0, 0), (0, 0), (0, 0))
    kc, vc = jnp.pad(chunk(k), pad), jnp.pad(chunk(v), pad)
    idx = jnp.arange(nc)[:, None] + jnp.arange(BAND_CHUNKS)[None, :]
    band = BAND_CHUNKS * CHUNK
    kb = kc[:, idx].reshape(b, nc, band, N_HEADS_ATT, HEAD_DIM)
    vb = vc[:, idx].reshape(b, nc, band, N_HEADS_ATT, HEAD_DIM)
    key_pos = (jnp.arange(nc)[:, None] - LEFT_CHUNKS) * CHUNK + jnp.arange(band)[None, :]
    o = band_attention(chunk(q), kb, vb, ATT_PAST, rel_bias, key_pos >= 0)
    return o.reshape(b, s, WIDTH_ATT)


def sample_attention(q, k, v, cache_k, cache_v, rel_bias):
    b, t = q.shape[:2]
    kf = jnp.concatenate([cache_k.astype(k.dtype), k], axis=1)[:, None]
    vf = jnp.concatenate([cache_v.astype(v.dtype), v], axis=1)[:, None]
    o = band_attention(q[:, None], kf, vf, cache_k.shape[1], rel_bias, None)
    return o[:, 0].reshape(b, t, WIDTH_ATT)


def trunk_layer(x, attend, hist_b, hist_c, g_mix, w_in_l, conv_b_w_l, conv_c_w_l, conv_c_b_l,
                ln_g, ln_b, w_out_l, g_ffn, w_gate, w_up, w_down):
    b, t, _ = x.shape
    h = rmsnorm(x, g_mix)
    z = h @ w_in_l
    q, k, v, sb, sc, sh, ca, cg = jnp.split(z, SPLITS, axis=-1)
    heads = lambda a: a.reshape(b, t, N_HEADS_ATT, HEAD_DIM)
    q, k, v = heads(q), heads(k), heads(v)
    y_att = attend(q, k, v)
    zb, new_hb = causal_dwconv(sc * sh, hist_b, conv_b_w_l)
    y_sc = sb * zb
    zc, new_hc = causal_dwconv(ca * jax.nn.sigmoid(cg), hist_c, conv_c_w_l)
    y_cv = jax.nn.silu(layernorm(zc + conv_c_b_l, ln_g, ln_b))
    x = x + jnp.concatenate([y_att, y_sc, y_cv], axis=-1) @ w_out_l
    h = rmsnorm(x, g_ffn)
    x = x + (jax.nn.silu(h @ w_gate) * (h @ w_up)) @ w_down
    return x, k, v, new_hb, new_hc


def setup_inputs(seed: int = 0) -> dict:
    key = jax.random.key(seed)
    ks = jax.random.split(key, 24)
    nrm = lambda k, shape, scale: jax.random.normal(k, shape, jnp.float32) * scale
    n_att = min(ATT_PAST, PAST_LEN)
    return {
        "x_prompt": nrm(ks[0], (BATCH, SEQ, D_MODEL), 1.0),
        "x_sample": nrm(ks[1], (DEC_BATCH, DEC_SEQ, D_MODEL), 1.0),
        "cache_attn_k": nrm(ks[2], (DEPTH, DEC_BATCH, n_att, N_HEADS_ATT, HEAD_DIM), 1.0),
        "cache_attn_v": nrm(ks[3], (DEPTH, DEC_BATCH, n_att, N_HEADS_ATT, HEAD_DIM), 1.0),
        "cache_conv_b": nrm(ks[4], (DEPTH, DEC_BATCH, SC_WIDTH - 1, WIDTH_SC), 1.0),
        "cache_conv_c": nrm(ks[5], (DEPTH, DEC_BATCH, CV_WIDTH - 1, WIDTH_CV), 0.5),
        "norm_mix_g": 1.0 + nrm(ks[6], (DEPTH, D_MODEL), 0.01),
        "w_in": nrm(ks[7], (DEPTH, D_MODEL, D_IN), D_MODEL ** -0.5),
        "rel_bias": nrm(ks[8], (DEPTH, N_HEADS_ATT, N_REL), 0.1),
        "conv_b_w": nrm(ks[9], (DEPTH, SC_WIDTH, WIDTH_SC), SC_WIDTH ** -0.5),
        "conv_c_w": nrm(ks[10], (DEPTH, CV_WIDTH, WIDTH_CV), CV_WIDTH ** -0.5),
        "conv_c_b": nrm(ks[11], (DEPTH, WIDTH_CV), 0.01),
        "ln_c_g": 1.0 + nrm(ks[12], (DEPTH, WIDTH_CV), 0.01),
        "ln_c_b": nrm(ks[13], (DEPTH, WIDTH_CV), 0.01),
        "w_out": nrm(ks[14], (DEPTH, D_MODEL, D_MODEL), D_MODEL ** -0.5),
        "norm_ffn_g": 1.0 + nrm(ks[15], (DEPTH, D_MODEL), 0.01),
        "w_ffn_gate": nrm(ks[16], (DEPTH, D_MODEL, D_FF), D_MODEL ** -0.5),
        "w_ffn_up": nrm(ks[17], (DEPTH, D_MODEL, D_FF), D_MODEL ** -0.5),
        "w_ffn_down": nrm(ks[18], (DEPTH, D_FF, D_MODEL), D_FF ** -0.5),
        "final_norm_g": 1.0 + nrm(ks[19], (D_MODEL,), 0.01),
    }


def reference(x_prompt, x_sample, cache_attn_k, cache_attn_v, cache_conv_b, cache_conv_c,
              norm_mix_g, w_in, rel_bias, conv_b_w, conv_c_w, conv_c_b, ln_c_g, ln_c_b,
              w_out, norm_ffn_g, w_ffn_gate, w_ffn_up, w_ffn_down, final_norm_g):
    xp, xs = x_prompt, x_sample
    bp, sp = xp.shape[0], xp.shape[1]
    n_keep = min(ATT_PAST, sp)
    pk, pv, pb, pc, sk, sv, sb_, sc_ = [], [], [], [], [], [], [], []
    for l in range(DEPTH):
        shared = (norm_mix_g[l], w_in[l], conv_b_w[l], conv_c_w[l], conv_c_b[l], ln_c_g[l], ln_c_b[l],
                  w_out[l], norm_ffn_g[l], w_ffn_gate[l], w_ffn_up[l], w_ffn_down[l])
        att_p = functools.partial(prompt_attention, rel_bias=rel_bias[l])
        hb0 = jnp.zeros((bp, SC_WIDTH - 1, WIDTH_SC), xp.dtype)
        hc0 = jnp.zeros((bp, CV_WIDTH - 1, WIDTH_CV), xp.dtype)
        xp, k_p, v_p, hb_p, hc_p = trunk_layer(xp, att_p, hb0, hc0, *shared)
        pk.append(k_p[:, sp - n_keep:])
        pv.append(v_p[:, sp - n_keep:])
        pb.append(hb_p)
        pc.append(hc_p)
        att_s = functools.partial(sample_attention, cache_k=cache_attn_k[l], cache_v=cache_attn_v[l],
                                  rel_bias=rel_bias[l])
        xs, k_s, v_s, hb_s, hc_s = trunk_layer(xs, att_s, cache_conv_b[l], cache_conv_c[l], *shared)
        sk.append(k_s)
        sv.append(v_s)
        sb_.append(hb_s)
        sc_.append(hc_s)
    y_prompt = rmsnorm(xp, final_norm_g)
    y_sample = rmsnorm(xs, final_norm_g)
    return (y_prompt, y_sample,
            jnp.stack(pk), jnp.stack(pv), jnp.stack(pb), jnp.stack(pc),
            jnp.stack(sk), jnp.stack(sv), jnp.stack(sb_), jnp.stack(sc_))
```

```python
import numpy as np
from contextlib import ExitStack
import concourse.bass as bass
import concourse.mybir as mybir
from concourse.bass_utils import run_bass_kernel_spmd

F32 = mybir.dt.float32
BF16 = mybir.dt.bfloat16
AF = mybir.ActivationFunctionType
ALU = mybir.AluOpType

L = 2
D = 4096
KC = 32
DFF = 11008
FC = 86
H = 16
TOK = 2112
TG = (1024, 1088)
EPS = 1e-6
SCALE = 128 ** -0.5
NEG = -30000.0
_STOP = None
_NB1 = None
_NOSTAT = False

OFF = {}
_o = 0
for _n, _sz in [("gmix", L * 32), ("gffn", L * 32), ("gfin", 32), ("cbw", L * 8 * 3), ("ccw", L * 8 * 31),
                ("ccb", L * 8), ("lng", L * 8), ("lnb", L * 8), ("histb", L * 2 * 8 * 2),
                ("histc", L * 2 * 8 * 30)]:
    OFF[_n] = _o
    _o += _sz
NPAR = _o

_QKV = [("q", h) for h in range(16)] + [("k", h) for h in range(16)] + [("v", h) for h in range(16)]
IN_ORDER = []
for _i in range(8):
    IN_ORDER += [("cg", _i), ("ca", _i)] + _QKV[6 * _i:6 * _i + 6] + [("sc", _i), ("sh", _i), ("sb", _i)]
ORIG_BLK = {"q": 0, "k": 16, "v": 32, "sb": 48, "sc": 56, "sh": 64, "ca": 72, "cg": 80}


class Tok:
    __slots__ = ("eng", "sem", "value", "marked", "is_dma")

    def __init__(self, eng, sem=None, value=None, is_dma=False):
        self.eng = eng
        self.sem = sem
        self.value = value
        self.marked = is_dma
        self.is_dma = is_dma


class Buf:
    __slots__ = ("w", "r")

    def __init__(self):
        self.w = None
        self.r = []


class GS:
    NPOOL = 20

    def __init__(self, nc, es):
        self.nc = nc
        self.eng_sem = {e: es.enter_context(nc.semaphore("sem_" + e)) for e in ("pe", "act", "dve", "pool")}
        self.cnt = {e: 0 for e in self.eng_sem}
        self.dma_sems = {q: [es.enter_context(nc.semaphore(f"dq_{q}{i}")) for i in range(self.NPOOL)]
                         for q in ("sp", "pool", "act")}
        self.dma_cnt = {q: [0] * self.NPOOL for q in self.dma_sems}
        self.dma_rr = {q: 0 for q in self.dma_sems}
        self.known = {e: {} for e in ("pe", "act", "dve", "pool", "sp")}
        self.bufs = []
        self.phase = 0

    def buf(self):
        b = Buf()
        self.bufs.append(b)
        return b


class Sched:
    ENGS = ("pe", "act", "dve", "pool", "sp")
    BLK = {"pe": "tensor", "act": "scalar", "dve": "vector", "pool": "gpsimd", "sp": "sync"}

    def __init__(self, gs):
        self.gs = gs
        self.nc = gs.nc
        self.ops = {e: [] for e in self.ENGS}
        self.used_dma = {e: {} for e in self.ENGS}

    def _deps(self, eng, reads, writes):
        deps = []
        for b in reads:
            if b.w is not None:
                deps.append(b.w)
        for b in writes:
            if b.w is not None:
                deps.append(b.w)
            deps.extend(b.r)
        out = []
        for d in deps:
            if (not d.is_dma) and d.eng == "pe" and eng == "pe":
                continue
            d.marked = True
            out.append(d)
        return out

    @staticmethod
    def _post(tok, reads, writes):
        for b in reads:
            b.r.append(tok)
        for b in writes:
            b.w = tok
            b.r = []

    def op(self, eng, emit, reads=(), writes=()):
        deps = self._deps(eng, reads, writes)
        tok = Tok(eng, self.gs.eng_sem[eng])
        self.ops[eng].append((deps, emit, tok))
        self._post(tok, reads, writes)
        return tok

    def dma(self, q, out, in_, reads=(), writes=()):
        gs = self.gs
        deps = self._deps(q, reads, writes)
        i = gs.dma_rr[q]
        gs.dma_rr[q] = (i + 1) % gs.NPOOL
        sem = gs.dma_sems[q][i]
        prev = gs.dma_cnt[q][i]
        gs.dma_cnt[q][i] = prev + 16
        tok = Tok(q, sem, prev + 16, is_dma=True)
        if prev > 0:
            deps.append(Tok(q, sem, prev, is_dma=True))
        self.used_dma[q][i] = (sem, prev + 16)
        self.ops[q].append((deps, (lambda e, o=out, s=in_: e.dma_start(out=o, in_=s)), tok))
        self._post(tok, reads, writes)
        return tok

    def run(self):
        gs = self.gs
        gs.phase += 1
        if _STOP is not None and gs.phase > _STOP:
            for b in gs.bufs:
                b.w = None
                b.r = []
            return
        for e in ("pe", "act", "dve", "pool"):
            for deps, emit, tok in self.ops[e]:
                if (not tok.is_dma) and tok.marked:
                    gs.cnt[e] += 1
                    tok.value = gs.cnt[e]
        with self.nc.Block() as blk:
            for e in self.ENGS:
                if not self.ops[e]:
                    continue

                def body(eh, e=e):
                    known = gs.known[e]
                    for deps, emit, tok in self.ops[e]:
                        need = {}
                        for d in deps:
                            k = id(d.sem)
                            if k not in need or need[k][1] < d.value:
                                need[k] = (d.sem, d.value)
                        for k, (sh, v) in need.items():
                            if known.get(k, 0) < v:
                                eh.wait_ge(sh, v)
                                known[k] = v
                        ins = emit(eh)
                        if tok.is_dma:
                            ins.then_inc(tok.sem, 16)
                        elif tok.marked:
                            ins.then_inc(tok.sem, 1)
                    for i, (sh, v) in self.used_dma[e].items():
                        k = id(sh)
                        if known.get(k, 0) < v:
                            eh.wait_ge(sh, v)
                            known[k] = v

                getattr(blk, self.BLK[e])(body)
        for b in gs.bufs:
            b.w = None
            b.r = []


class T:
    def __init__(self, gs, h, nsub=0):
        self.h = h
        self.buf = gs.buf()
        self.sub = [gs.buf() for _ in range(nsub)]


def tiles_of(n):
    out = []
    c = 0
    while c < n:
        w = min(512, n - c)
        out.append((c, w))
        c += w
    return out


def build_program():
    nc = bass.Bass("TRN2", target_bir_lowering=False)
    di = lambda n, s, dt=F32: nc.dram_tensor(n, s, dt, kind="ExternalInput").ap()
    do = lambda n, s, dt=F32: nc.dram_tensor(n, s, dt, kind="ExternalOutput").ap()
    dx = lambda n, s, dt: nc.dram_tensor(n, s, dt).ap()

    xT = di("xT", [D, TOK])
    w_in = di("w_in", [L * 88, 128, 4096])
    w_out = di("w_out", [L * 32, 128, 4096])
    w_gu = di("w_gu", [L * 172, 128, 4096])
    w_dn = di("w_dn", [L * 32, 128, DFF])
    par_d = di("par", [128, NPAR])
    biasp_d = di("biasp", [L * 16, 128, 640])
    biass_d = di("biass", [L * 16, 128, 160])
    ckT_d = di("ckT", [L * 2 * 16, 128, 512])
    cv_d = di("cv", [L * 2, 512, 16, 128])
    ident_d = di("ident", [128, 128])

    yT = do("yT", [D, TOK])
    kTo = do("kTo", [L * 2048, 576])
    vTo = do("vTo", [L * 2048, 576])
    cbo_d = do("cbo", [128, L * 8 * 6])
    cco_d = do("cco", [128, L * 8 * 90])

    x_d = dx("x_d", [D, TOK], F32)
    q_d = dx("q_d", [16 * 128, 1088], BF16)
    k_d = dx("k_d", [L * 16 * 128, TOK], BF16)
    v_d = dx("v_d", [L * 16 * 128, TOK], BF16)
    cat_d = dx("cat_d", [2048, 1088], BF16)
    zc_d = dx("zc_d", [1024, 1088], F32)
    hid_d = dx("hid_d", [DFF, 1088], BF16)

    with ExitStack() as top:
        gs = GS(nc, top)
        uid = [0]

        def uname(n):
            uid[0] += 1
            return f"{n}_u{uid[0]}"

        sb = lambda es, n, s, dt, nsub=0: T(gs, es.enter_context(nc.sbuf_tensor(uname(n), s, dt)), nsub)
        ps = lambda es, n, s, dt=F32: T(gs, es.enter_context(nc.psum_tensor(uname(n), s, dt)))

        par = sb(top, "par", [128, NPAR], F32)
        ones_b = sb(top, "ones_b", [128, 128], BF16)
        ident_b = sb(top, "ident_b", [128, 128], BF16)
        rstd_b = sb(top, "rstd_b", [128, 1088], F32)
        mean_b = sb(top, "mean_b", [128, 1088], F32)
        halo_b = sb(top, "halo_b", [128, L * 8 * 2], F32)
        halo_c = sb(top, "halo_c", [128, L * 8 * 30], F32)
        cbo_sb = sb(top, "cbo_sb", [128, L * 8 * 6], F32)
        cco_sb = sb(top, "cco_sb", [128, L * 8 * 90], F32)

        dbuf = {}

        def DB(*key):
            if key not in dbuf:
                dbuf[key] = gs.buf()
            return dbuf[key]

        pcol = lambda name, idx: par.h[:, OFF[name] + idx: OFF[name] + idx + 1]

        S = Sched(gs)
        S.dma("sp", par.h[:], par_d, writes=[par.buf])
        S.dma("pool", ident_b.h[:], ident_d, writes=[ident_b.buf])
        S.op("dve", lambda e: e.memset(ones_b.h[:], 1.0), writes=[ones_b.buf])
        S.run()

        def rstd_from_stats(S, stat_tiles, tl, col0, tmp):
            for (c, n), st in zip(tl, stat_tiles):
                S.op("dve", lambda e, c=c, n=n, st=st: e.tensor_scalar(
                    out=tmp.h[:, c:c + n], in0=st.h[:, 0:n], scalar1=1.0 / D, scalar2=EPS,
                    op0=ALU.mult, op1=ALU.add), reads=[st.buf], writes=[tmp.buf])
                S.op("act", lambda e, c=c, n=n: e.activation(out=tmp.h[:, c:c + n], in_=tmp.h[:, c:c + n],
                                                             func=AF.Sqrt), reads=[tmp.buf], writes=[tmp.buf])
                S.op("dve", lambda e, c=c, n=n: e.reciprocal(out=rstd_b.h[:, col0 + c: col0 + c + n],
                                                             in_=tmp.h[:, c:c + n]),
                     reads=[tmp.buf], writes=[rstd_b.buf])

        def stat_row(S, sts, tl_, src, hl, first, last, defer_to=None):
            if _NOSTAT:
                return
            hi, lo = hl
            S.op("act", lambda e: e.activation(out=hi.h[:], in_=src.h[:], func=AF.Copy),
                 reads=[src.buf], writes=[hi.buf])
            S.op("pool", lambda e: e.tensor_tensor(out=lo.h[:], in0=src.h[:], in1=hi.h[:], op=ALU.subtract),
                 reads=[src.buf, hi.buf], writes=[lo.buf])
            def pe_part():
                for (c, n), st_ in zip(tl_, sts):
                    def emit(e, c=c, n=n, st_=st_):
                        e.matmul(st_.h[:, 0:n], lhsT=ones_b.h[:], rhs=hi.h[:, c:c + n], start=first, stop=False)
                        return e.matmul(st_.h[:, 0:n], lhsT=ones_b.h[:], rhs=lo.h[:, c:c + n], start=False, stop=last)
                    S.op("pe", emit, reads=[hi.buf, lo.buf, ones_b.buf], writes=[st_.buf])

            if defer_to is None:
                pe_part()
            else:
                S.deferred.append((defer_to, pe_part))

        def run_gemm(S, A3, a_bufs, kcn, tl, nblocks, wsrc, ring, pss, epi, epi_end, esz, P=1):
            R = len(ring)
            kp = kcn // P
            total = nblocks * P
            S.deferred = []
            nl = [0]

            def load(j):
                slot = ring[j % R]
                S.dma("pool", slot.h[:, 0:kp * 128].rearrange("p (c e) -> p c e", e=esz),
                      wsrc(j // P, j % P).rearrange("p (c e) -> p c e", e=esz), writes=[slot.buf])

            def flush(i):
                keep = []
                for (tgt, th) in S.deferred:
                    if tgt <= i:
                        th()
                    else:
                        keep.append((tgt, th))
                S.deferred = keep

            pi = 0
            for i in range(nblocks):
                while nl[0] < total and nl[0] < i * P + R:
                    load(nl[0])
                    nl[0] += 1
                slots = [ring[(i * P + p) % R] for p in range(P)]
                w3s = [sl.h[:, 0:kp * 128].rearrange("p (k n) -> p k n", n=128) for sl in slots]
                for ti, (c, n) in enumerate(tl):
                    pt = pss[pi % len(pss)]
                    pi += 1

                    def emit(e, w3s=w3s, pt=pt, c=c, n=n):
                        for kc in range(kcn):
                            ins = e.matmul(pt.h[:, 0:n], lhsT=w3s[kc // kp][:, kc % kp, :], rhs=A3[:, kc, c:c + n],
                                           start=(kc == 0), stop=(kc == kcn - 1))
                        return ins

                    S.op("pe", emit, reads=list(a_bufs) + [sl.buf for sl in slots], writes=[pt.buf])
                    epi(i, ti, c, n, pt)
                flush(i)
                epi_end(i)
            flush(10 ** 9)

        for g in range(2):
            Tg = TG[g]
            tl = tiles_of(Tg)
            G0 = 1024 * g

            with ExitStack() as es:
                S = Sched(gs)
                xr = [sb(es, f"s0_xr{i}", [128, Tg], F32) for i in range(3)]
                sq = [sb(es, f"s0_sq{i}", [128, Tg], F32) for i in range(2)]
                tmp = sb(es, "s0_tmp", [128, Tg], F32)
                hl = [(sb(es, f"s0_hi{i}", [128, Tg], BF16), sb(es, f"s0_lo{i}", [128, Tg], BF16)) for i in range(2)]
                st = [ps(es, f"s0_st{i}", [128, 512]) for i in range(len(tl))]
                for blk in range(32):
                    x_ = xr[blk % 3]
                    s_ = sq[blk % 2]
                    S.dma("sp", x_.h[:], xT[blk * 128:(blk + 1) * 128, G0:G0 + Tg], writes=[x_.buf])
                    S.op("act", lambda e, x_=x_, s_=s_: e.activation(out=s_.h[:], in_=x_.h[:], func=AF.Square),
                         reads=[x_.buf], writes=[s_.buf])
                    stat_row(S, st, tl, s_, hl[blk % 2], blk == 0, blk == 31)
                rstd_from_stats(S, st, tl, 0, tmp)
                S.run()

            for l in range(L):
                x_src = xT if l == 0 else x_d

                with ExitStack() as es:
                    S = Sched(gs)
                    A = sb(es, "g1_A", [128, 32, Tg], BF16, nsub=32)
                    ring = [sb(es, f"g1_w{i}", [128, 4096], BF16) for i in range(5)]
                    hl = [(sb(es, f"g1_hi{i}", [128, Tg], BF16), sb(es, f"g1_lo{i}", [128, Tg], BF16)) for i in range(2)]
                    xr = [sb(es, f"g1_xr{i}", [128, Tg], F32) for i in range(2)]
                    stA = sb(es, "g1_stA", [128, Tg], F32)
                    UL = 1094 if g == 1 else 1026
                    CL = 1178 if g == 1 else 1054
                    ubuf = sb(es, "g1_ubuf", [128, UL], F32)
                    cbuf = sb(es, "g1_cbuf", [128, CL], F32)
                    acc = [sb(es, f"g1_acc{i}", [128, CL], F32) for i in range(2)]
                    zcrow = [sb(es, f"g1_zc{i}", [128, Tg], F32) for i in range(2)]
                    o16 = [sb(es, f"g1_o16{i}", [128, Tg], BF16) for i in range(3)]
                    okv = [sb(es, f"g1_okv{i}", [128, 576], F32) for i in range(2)]
                    pss = [ps(es, f"g1_ps{i}", [128, 512]) for i in range(4)]
                    st = [ps(es, f"g1_st{i}", [128, 512]) for i in range(len(tl))]

                    for blk in range(32):
                        x_ = xr[blk % 2]
                        S.dma("sp", x_.h[:], x_src[blk * 128:(blk + 1) * 128, G0:G0 + Tg],
                              reads=[DB("x", blk)], writes=[x_.buf])
                        S.op("dve", lambda e, x_=x_, blk=blk: e.scalar_tensor_tensor(
                            out=A.h[:, blk, :], in0=x_.h[:], scalar=pcol("gmix", l * 32 + blk),
                            in1=rstd_b.h[:, 0:Tg], op0=ALU.mult, op1=ALU.mult),
                             reads=[x_.buf, rstd_b.buf], writes=[A.sub[blk]])

                    if g == 0:
                        S.op("dve", lambda e: e.memset(ubuf.h[:, 0:2], 0.0), writes=[ubuf.buf])
                        S.op("dve", lambda e: e.memset(cbuf.h[:, 0:30], 0.0), writes=[cbuf.buf])
                    state = {"o16": 0, "okv": 0, "acc": 0, "zc": 0, "nC": 0}

                    def seg_map(c, n, upos, spos):
                        if c < 1024:
                            return [(0, n, upos + c)]
                        return [(0, 32, spos[0]), (32, 64, spos[1])]

                    def epi(i, ti, c, n, pt):
                        kind, idx = IN_ORDER[i]
                        if kind in ("q", "k", "v"):
                            o_ = o16[state["o16"] % 3]
                            S.op("act", lambda e: e.activation(out=o_.h[:, c:c + n], in_=pt.h[:, 0:n], func=AF.Copy),
                                 reads=[pt.buf], writes=[o_.buf])
                            if kind != "q" and g == 1 and ti >= 1:
                                ok = okv[state["okv"] % 2]
                                oc = 0 if ti == 1 else 512
                                S.op("act", lambda e: e.activation(out=ok.h[:, oc:oc + n], in_=pt.h[:, 0:n], func=AF.Copy),
                                     reads=[pt.buf], writes=[ok.buf])
                        elif kind == "sc":
                            S.op("act", lambda e: e.activation(out=stA.h[:, c:c + n], in_=pt.h[:, 0:n], func=AF.Copy),
                                 reads=[pt.buf], writes=[stA.buf])
                        elif kind == "sh":
                            for (a, b, dst) in seg_map(c, n, 2, (1028, 1062)):
                                S.op("dve", lambda e, a=a, b=b, dst=dst: e.tensor_tensor(
                                    out=ubuf.h[:, dst:dst + (b - a)], in0=pt.h[:, a:b], in1=stA.h[:, c + a:c + b],
                                    op=ALU.mult), reads=[pt.buf, stA.buf], writes=[ubuf.buf])
                        elif kind == "sb":
                            o_ = o16[state["o16"] % 3]
                            ac = acc[state["acc"] % 2]
                            for (a, b, src) in seg_map(c, n, 0, (1026, 1060)):
                                S.op("dve", lambda e, a=a, b=b, src=src: e.tensor_tensor(
                                    out=o_.h[:, c + a:c + b], in0=pt.h[:, a:b], in1=ac.h[:, src:src + (b - a)],
                                    op=ALU.mult), reads=[pt.buf, ac.buf], writes=[o_.buf])
                        elif kind == "cg":
                            S.op("act", lambda e: e.activation(out=stA.h[:, c:c + n], in_=pt.h[:, 0:n],
                                                               func=AF.Sigmoid),
                                 reads=[pt.buf], writes=[stA.buf])
                        elif kind == "ca":
                            for (a, b, dst) in seg_map(c, n, 30, (1084, 1146)):
                                S.op("dve", lambda e, a=a, b=b, dst=dst: e.tensor_tensor(
                                    out=cbuf.h[:, dst:dst + (b - a)], in0=pt.h[:, a:b], in1=stA.h[:, c + a:c + b],
                                    op=ALU.mult), reads=[pt.buf, stA.buf], writes=[cbuf.buf])

                    def epi_end(i):
                        kind, idx = IN_ORDER[i]
                        if kind in ("q", "k", "v"):
                            o_ = o16[state["o16"] % 3]
                            state["o16"] += 1
                            if kind == "q":
                                S.dma("sp", q_d[idx * 128:(idx + 1) * 128, 0:Tg], o_.h[:], reads=[o_.buf],
                                      writes=[DB("q", idx)])
                            else:
                                dst = k_d if kind == "k" else v_d
                                r0 = (l * 16 + idx) * 128
                                S.dma("sp", dst[r0:r0 + 128, G0:G0 + Tg], o_.h[:], reads=[o_.buf],
                                      writes=[DB(kind, l, idx)])
                                if g == 1:
                                    ok = okv[state["okv"] % 2]
                                    state["okv"] += 1
                                    od = kTo if kind == "k" else vTo
                                    r1 = l * 2048 + idx * 128
                                    S.dma("sp", od[r1:r1 + 128, :], ok.h[:], reads=[ok.buf])
                        elif kind == "sh":
                            if g == 1:
                                hb = OFF["histb"] + (l * 2 * 8 + idx) * 2
                                S.op("dve", lambda e: e.tensor_copy(out=ubuf.h[:, 0:2],
                                                                    in_=halo_b.h[:, (l * 8 + idx) * 2:(l * 8 + idx) * 2 + 2]),
                                     reads=[halo_b.buf], writes=[ubuf.buf])
                                for s in range(2):
                                    hb = OFF["histb"] + ((l * 2 + s) * 8 + idx) * 2
                                    S.op("dve", lambda e, s=s, hb=hb: e.tensor_copy(
                                        out=ubuf.h[:, 1026 + 34 * s:1028 + 34 * s], in_=par.h[:, hb:hb + 2]),
                                         writes=[ubuf.buf])
                            ac = acc[state["acc"] % 2]
                            n_out = UL - 2
                            w = lambda j: pcol("cbw", (l * 8 + idx) * 3 + j)
                            S.op("dve", lambda e: e.tensor_scalar(out=ac.h[:, 0:n_out], in0=ubuf.h[:, 0:n_out],
                                                                  scalar1=w(0), scalar2=None, op0=ALU.mult),
                                 reads=[ubuf.buf], writes=[ac.buf])
                            for j in (1, 2):
                                S.op("dve", lambda e, j=j: e.scalar_tensor_tensor(
                                    out=ac.h[:, 0:n_out], in0=ubuf.h[:, j:j + n_out], scalar=w(j),
                                    in1=ac.h[:, 0:n_out], op0=ALU.mult, op1=ALU.add),
                                     reads=[ubuf.buf, ac.buf], writes=[ac.buf])
                            if g == 0:
                                S.op("dve", lambda e: e.tensor_copy(
                                    out=halo_b.h[:, (l * 8 + idx) * 2:(l * 8 + idx) * 2 + 2], in_=ubuf.h[:, 1024:1026]),
                                     reads=[ubuf.buf], writes=[halo_b.buf])
                            else:
                                for s3, src in enumerate((1024, 1058, 1092)):
                                    o0 = (l * 8 + idx) * 6 + s3 * 2
                                    S.op("dve", lambda e, o0=o0, src=src: e.tensor_copy(
                                        out=cbo_sb.h[:, o0:o0 + 2], in_=ubuf.h[:, src:src + 2]),
                                         reads=[ubuf.buf], writes=[cbo_sb.buf])
                        elif kind == "sb":
                            o_ = o16[state["o16"] % 3]
                            state["o16"] += 1
                            state["acc"] += 1
                            S.dma("sp", cat_d[idx * 128:(idx + 1) * 128, 0:Tg], o_.h[:], reads=[o_.buf],
                                  writes=[DB("cat", idx)])
                        elif kind == "ca":
                            if g == 1:
                                S.op("dve", lambda e: e.tensor_copy(
                                    out=cbuf.h[:, 0:30], in_=halo_c.h[:, (l * 8 + idx) * 30:(l * 8 + idx) * 30 + 30]),
                                     reads=[halo_c.buf], writes=[cbuf.buf])
                                for s in range(2):
                                    hc = OFF["histc"] + ((l * 2 + s) * 8 + idx) * 30
                                    S.op("dve", lambda e, s=s, hc=hc: e.tensor_copy(
                                        out=cbuf.h[:, 1054 + 62 * s:1084 + 62 * s], in_=par.h[:, hc:hc + 30]),
                                         writes=[cbuf.buf])
                            ac = acc[state["acc"] % 2]
                            state["acc"] += 1
                            n_out = CL - 30
                            w = lambda j: pcol("ccw", (l * 8 + idx) * 31 + j)
                            S.op("dve", lambda e: e.tensor_scalar(
                                out=ac.h[:, 0:n_out], in0=cbuf.h[:, 0:n_out], scalar1=w(0),
                                scalar2=pcol("ccb", l * 8 + idx), op0=ALU.mult, op1=ALU.add),
                                 reads=[cbuf.buf], writes=[ac.buf])
                            for j in range(1, 31):
                                S.op("dve", lambda e, j=j: e.scalar_tensor_tensor(
                                    out=ac.h[:, 0:n_out], in0=cbuf.h[:, j:j + n_out], scalar=w(j),
                                    in1=ac.h[:, 0:n_out], op0=ALU.mult, op1=ALU.add),
                                     reads=[cbuf.buf, ac.buf], writes=[ac.buf])
                            if g == 0:
                                S.op("dve", lambda e: e.tensor_copy(
                                    out=halo_c.h[:, (l * 8 + idx) * 30:(l * 8 + idx) * 30 + 30],
                                    in_=cbuf.h[:, 1024:1054]), reads=[cbuf.buf], writes=[halo_c.buf])
                            else:
                                for s3, src in enumerate((1024, 1086, 1148)):
                                    o0 = (l * 8 + idx) * 90 + s3 * 30
                                    S.op("dve", lambda e, o0=o0, src=src: e.tensor_copy(
                                        out=cco_sb.h[:, o0:o0 + 30], in_=cbuf.h[:, src:src + 30]),
                                         reads=[cbuf.buf], writes=[cco_sb.buf])
                            zr = zcrow[state["zc"] % 2]
                            state["zc"] += 1
                            segs = [(0, 1024, 0)] + ([(1024, 32, 1054), (1056, 32, 1116)] if g == 1 else [])
                            for (dst, n_, src) in segs:
                                S.op("act", lambda e, dst=dst, n_=n_, src=src: e.activation(
                                    out=zr.h[:, dst:dst + n_], in_=ac.h[:, src:src + n_], func=AF.Copy),
                                     reads=[ac.buf], writes=[zr.buf])
                            stat_row(S, st, tl, zr, hl[state["nC"] % 2], state["nC"] == 0, state["nC"] == 7, defer_to=i + 4)
                            state["nC"] += 1
                            S.dma("sp", zc_d[idx * 128:(idx + 1) * 128, 0:Tg], zr.h[:], reads=[zr.buf],
                                  writes=[DB("zc", idx)])

                    run_gemm(S, A.h, A.sub, 32, tl, 88 if _NB1 is None else _NB1, lambda i, p: w_in[l * 88 + i], ring, pss, epi, epi_end, 2048)
                    for (c, n), stt in zip(tl, st):
                        S.op("act", lambda e, c=c, n=n, stt=stt: e.activation(
                            out=mean_b.h[:, c:c + n], in_=stt.h[:, 0:n], func=AF.Copy, scale=1.0 / 1024),
                             reads=[stt.buf], writes=[mean_b.buf])
                    S.run()

                with ExitStack() as es:
                    S = Sched(gs)
                    dd = [sb(es, f"ln_d{i}", [128, Tg], F32) for i in range(8)]
                    xr = [sb(es, f"ln_x{i}", [128, Tg], F32) for i in range(2)]
                    sq = [sb(es, f"ln_sq{i}", [128, Tg], F32) for i in range(2)]
                    tmp = sb(es, "ln_tmp", [128, Tg], F32)
                    hl = [(sb(es, f"ln_hi{i}", [128, Tg], BF16), sb(es, f"ln_lo{i}", [128, Tg], BF16)) for i in range(2)]
                    rs = sb(es, "ln_rs", [128, Tg], F32)
                    t1 = [sb(es, f"ln_t{i}", [128, Tg], F32) for i in range(2)]
                    o16 = [sb(es, f"ln_o{i}", [128, Tg], BF16) for i in range(2)]
                    st = [ps(es, f"ln_st{i}", [128, 512]) for i in range(len(tl))]
                    for blk in range(8):
                        x_ = xr[blk % 2]
                        s_ = sq[blk % 2]
                        d_ = dd[blk]
                        S.dma("sp", x_.h[:], zc_d[blk * 128:(blk + 1) * 128, 0:Tg], reads=[DB("zc", blk)],
                              writes=[x_.buf])
                        S.op("dve", lambda e, x_=x_, d_=d_: e.tensor_tensor(out=d_.h[:], in0=x_.h[:],
                                                                           in1=mean_b.h[:, 0:Tg], op=ALU.subtract),
                             reads=[x_.buf, mean_b.buf], writes=[d_.buf])
                        S.op("act", lambda e, s_=s_, d_=d_: e.activation(out=s_.h[:], in_=d_.h[:], func=AF.Square),
                             reads=[d_.buf], writes=[s_.buf])
                        stat_row(S, st, tl, s_, hl[blk % 2], blk == 0, blk == 7)
                    for (c, n), stt in zip(tl, st):
                        S.op("dve", lambda e, c=c, n=n, stt=stt: e.tensor_scalar(
                            out=tmp.h[:, c:c + n], in0=stt.h[:, 0:n], scalar1=1.0 / 1024, scalar2=EPS,
                            op0=ALU.mult, op1=ALU.add), reads=[stt.buf], writes=[tmp.buf])
                    S.op("act", lambda e: e.activation(out=tmp.h[:], in_=tmp.h[:], func=AF.Sqrt),
                         reads=[tmp.buf], writes=[tmp.buf])
                    S.op("dve", lambda e: e.reciprocal(out=rs.h[:], in_=tmp.h[:]), reads=[tmp.buf], writes=[rs.buf])
                    for blk in range(8):
                        d_ = dd[blk]
                        t_ = t1[blk % 2]
                        o_ = o16[blk % 2]
                        S.op("dve", lambda e, d_=d_, t_=t_: e.tensor_tensor(out=t_.h[:], in0=d_.h[:], in1=rs.h[:],
                                                                           op=ALU.mult),
                             reads=[d_.buf, rs.buf], writes=[t_.buf])
                        S.op("dve", lambda e, t_=t_, blk=blk: e.tensor_scalar(
                            out=t_.h[:], in0=t_.h[:], scalar1=pcol("lng", l * 8 + blk), scalar2=pcol("lnb", l * 8 + blk),
                            op0=ALU.mult, op1=ALU.add), reads=[t_.buf], writes=[t_.buf])
                        S.op("act", lambda e, t_=t_, o_=o_: e.activation(out=o_.h[:], in_=t_.h[:], func=AF.Silu),
                             reads=[t_.buf], writes=[o_.buf])
                        S.dma("sp", cat_d[(8 + blk) * 128:(9 + blk) * 128, 0:Tg], o_.h[:], reads=[o_.buf],
                              writes=[DB("cat", 8 + blk)])
                    S.run()

                with ExitStack() as es2:
                    A = sb(es2, "g2_A", [128, 32, Tg], BF16, nsub=32)
                    with ExitStack() as es:
                        S = Sched(gs)
                        KW = 1024 if g == 0 else 1600
                        C0 = 0 if g == 0 else 512
                        NT = KW // 128 if g == 0 else 12
                        qh = [sb(es, f"at_q{i}", [128, Tg], BF16) for i in range(2)]
                        kh = [sb(es, f"at_k{i}", [128, KW], BF16) for i in range(2)]
                        vh = [sb(es, f"at_v{i}", [128, KW], BF16) for i in range(2)]
                        vt = [sb(es, f"at_vt{i}", [128, NT, 128], BF16) for i in range(2)]
                        bp = [sb(es, f"at_bp{i}", [128, 640], F32) for i in range(2)]
                        ssb = [sb(es, f"at_s{i}", [128, 640], F32) for i in range(2)]
                        eb = [sb(es, f"at_e{i}", [128, 640], BF16) for i in range(2)]
                        rd = [sb(es, f"at_rd{i}", [128, 128], F32) for i in range(2)]
                        sps = [ps(es, f"at_sps{i}", [128, 1024]) for i in range(2)]
                        ods = [ps(es, f"at_od{i}", [128, 512]) for i in range(2)]
                        trp = [ps(es, f"at_tr{i}", [128, 1024], BF16) for i in range(2)]
                        if g == 1:
                            kcs = [[sb(es, f"at_kc{i}{s}", [128, 512], BF16) for s in range(2)] for i in range(2)]
                            vcs = [[sb(es, f"at_vc{i}{s}", [128, 4, 128], BF16) for s in range(2)] for i in range(2)]
                            bs = [sb(es, f"at_bs{i}", [128, 160], F32) for i in range(2)]
                            vn = [[sb(es, f"at_vn{i}{s}", [32, 128], BF16) for s in range(2)] for i in range(2)]
                        cnt = {"sp": 0, "od": 0, "tr": 0, "x": 0}

                        def load_head(h):
                            p = h % 2
                            r0 = (l * 16 + h) * 128
                            S.dma("sp", qh[p].h[:], q_d[h * 128:(h + 1) * 128, 0:Tg], reads=[DB("q", h)],
                                  writes=[qh[p].buf])
                            S.dma("sp", kh[p].h[:], k_d[r0:r0 + 128, C0:C0 + KW], reads=[DB("k", l, h)],
                                  writes=[kh[p].buf])
                            S.dma("sp", vh[p].h[:], v_d[r0:r0 + 128, C0:C0 + KW], reads=[DB("v", l, h)],
                                  writes=[vh[p].buf])
                            S.dma("sp", bp[p].h[:], biasp_d[l * 16 + h], writes=[bp[p].buf])
                            if g == 1:
                                S.dma("sp", bs[p].h[:], biass_d[l * 16 + h], writes=[bs[p].buf])
                                for s in range(2):
                                    S.dma("pool", kcs[p][s].h[:], ckT_d[(l * 2 + s) * 16 + h], writes=[kcs[p][s].buf])
                                    S.dma("pool", vcs[p][s].h[:],
                                          cv_d[l * 2 + s][:, h, :].rearrange("(j k) d -> k j d", k=128),
                                          writes=[vcs[p][s].buf])

                        load_head(0)
                        for h in range(16):
                            p = h % 2
                            if h + 1 < 16:
                                load_head(h + 1)
                            for t0 in range(0, NT, 4):
                                nt = min(4, NT - t0)
                                tp = trp[cnt["tr"] % 2]
                                cnt["tr"] += 1

                                def emit(e, t0=t0, nt=nt, tp=tp, p=p):
                                    for t in range(nt):
                                        ins = e.transpose(tp.h[:, t * 128:(t + 1) * 128],
                                                          vh[p].h[:, (t0 + t) * 128:(t0 + t + 1) * 128], ident_b.h[:])
                                    return ins

                                S.op("pe", emit, reads=[vh[p].buf, ident_b.buf], writes=[tp.buf])
                                S.op("act", lambda e, t0=t0, nt=nt, tp=tp, p=p: e.activation(
                                    out=vt[p].h[:, t0:t0 + nt, :],
                                    in_=tp.h[:, 0:nt * 128].rearrange("p (t d) -> p t d", d=128), func=AF.Copy),
                                     reads=[tp.buf], writes=[vt[p].buf])
                            if g == 1:
                                for s in range(2):
                                    tp = trp[cnt["tr"] % 2]
                                    cnt["tr"] += 1
                                    S.op("pe", lambda e, tp=tp, s=s, p=p: e.transpose(
                                        tp.h[0:32, 0:128], vh[p].h[:, 1536 + 32 * s:1568 + 32 * s], ident_b.h[:]),
                                         reads=[vh[p].buf, ident_b.buf], writes=[tp.buf])
                                    S.op("act", lambda e, tp=tp, s=s, p=p: e.activation(
                                        out=vn[p][s].h[:], in_=tp.h[0:32, 0:128], func=AF.Copy),
                                         reads=[tp.buf], writes=[vn[p][s].buf])
                            for ml in range(8):
                                m = 8 * g + ml
                                j0 = max(0, 4 - m)
                                sp_ = sps[cnt["sp"] % 2]
                                cnt["sp"] += 1
                                od = ods[cnt["od"] % 2]
                                cnt["od"] += 1
                                x = cnt["x"] % 2
                                cnt["x"] += 1
                                qa = qh[p].h[:, ml * 128:(ml + 1) * 128]

                                def emit_s(e, sp_=sp_, j0=j0, m=m, qa=qa, p=p):
                                    for j in range(j0, 5):
                                        kc0 = 128 * (m - 4 + j) - C0
                                        ins = e.matmul(sp_.h[:, j * 128:(j + 1) * 128], lhsT=kh[p].h[:, kc0:kc0 + 128],
                                                       rhs=qa, start=True, stop=True)
                                    return ins

                                S.op("pe", emit_s, reads=[kh[p].buf, qh[p].buf], writes=[sp_.buf])
                                a0, a1 = j0 * 128, 640
                                S.op("dve", lambda e, sp_=sp_, x=x, p=p, a0=a0, a1=a1: e.scalar_tensor_tensor(
                                    out=ssb[x].h[:, a0:a1], in0=sp_.h[:, a0:a1], scalar=SCALE, in1=bp[p].h[:, a0:a1],
                                    op0=ALU.mult, op1=ALU.add), reads=[sp_.buf, bp[p].buf], writes=[ssb[x].buf])
                                S.op("act", lambda e, x=x, a0=a0, a1=a1: e.activation(
                                    out=eb[x].h[:, a0:a1], in_=ssb[x].h[:, a0:a1], func=AF.Exp),
                                     reads=[ssb[x].buf], writes=[eb[x].buf])

                                def emit_o(e, od=od, j0=j0, m=m, x=x, p=p):
                                    for j in range(j0, 5):
                                        ti = m - 4 + j - C0 // 128
                                        e.matmul(od.h[:, 0:128], lhsT=vt[p].h[:, ti, :], rhs=eb[x].h[:, j * 128:(j + 1) * 128],
                                                 start=(j == j0), stop=(j == 4))
                                    for j in range(j0, 5):
                                        ins = e.matmul(od.h[:, 128:256], lhsT=ones_b.h[:], rhs=eb[x].h[:, j * 128:(j + 1) * 128],
                                                       start=(j == j0), stop=(j == 4))
                                    return ins

                                S.op("pe", emit_o, reads=[vt[p].buf, eb[x].buf, ones_b.buf], writes=[od.buf])
                                S.op("dve", lambda e, od=od, x=x: e.reciprocal(out=rd[x].h[:], in_=od.h[:, 128:256]),
                                     reads=[od.buf], writes=[rd[x].buf])
                                S.op("dve", lambda e, od=od, x=x, h=h, ml=ml: e.tensor_tensor(
                                    out=A.h[:, h, ml * 128:(ml + 1) * 128], in0=od.h[:, 0:128], in1=rd[x].h[:],
                                    op=ALU.mult), reads=[od.buf, rd[x].buf], writes=[A.sub[h]])
                            if g == 1:
                                for s in range(2):
                                    sp_ = sps[cnt["sp"] % 2]
                                    cnt["sp"] += 1
                                    od = ods[cnt["od"] % 2]
                                    cnt["od"] += 1
                                    x = cnt["x"] % 2
                                    cnt["x"] += 1
                                    qa = qh[p].h[:, 1024 + 32 * s:1056 + 32 * s]
                                    kn = kh[p].h[:, 1536 + 32 * s:1568 + 32 * s]

                                    def emit_s(e, sp_=sp_, qa=qa, kn=kn, p=p, s=s):
                                        for jt in range(4):
                                            e.matmul(sp_.h[:, jt * 32:(jt + 1) * 32],
                                                     lhsT=kcs[p][s].h[:, jt * 128:(jt + 1) * 128], rhs=qa,
                                                     start=True, stop=True)
                                        return e.matmul(sp_.h[0:32, 128:160], lhsT=kn, rhs=qa, start=True, stop=True)

                                    S.op("pe", emit_s, reads=[kcs[p][s].buf, kh[p].buf, qh[p].buf], writes=[sp_.buf])
                                    for (r, a0, a1) in ((128, 0, 128), (32, 128, 160)):
                                        S.op("dve", lambda e, sp_=sp_, x=x, p=p, r=r, a0=a0, a1=a1: e.scalar_tensor_tensor(
                                            out=ssb[x].h[0:r, a0:a1], in0=sp_.h[0:r, a0:a1], scalar=SCALE,
                                            in1=bs[p].h[0:r, a0:a1], op0=ALU.mult, op1=ALU.add),
                                             reads=[sp_.buf, bs[p].buf], writes=[ssb[x].buf])
                                        S.op("act", lambda e, x=x, r=r, a0=a0, a1=a1: e.activation(
                                            out=eb[x].h[0:r, a0:a1], in_=ssb[x].h[0:r, a0:a1], func=AF.Exp),
                                             reads=[ssb[x].buf], writes=[eb[x].buf])

                                    def emit_o(e, od=od, x=x, p=p, s=s):
                                        for jt in range(4):
                                            e.matmul(od.h[:, 0:32], lhsT=vcs[p][s].h[:, jt, :],
                                                     rhs=eb[x].h[:, jt * 32:(jt + 1) * 32], start=(jt == 0), stop=False)
                                        e.matmul(od.h[:, 0:32], lhsT=vn[p][s].h[:], rhs=eb[x].h[0:32, 128:160],
                                                 start=False, stop=True)
                                        for jt in range(4):
                                            e.matmul(od.h[:, 128:160], lhsT=ones_b.h[:],
                                                     rhs=eb[x].h[:, jt * 32:(jt + 1) * 32], start=(jt == 0), stop=False)
                                        return e.matmul(od.h[:, 128:160], lhsT=ones_b.h[0:32, :],
                                                        rhs=eb[x].h[0:32, 128:160], start=False, stop=True)

                                    S.op("pe", emit_o, reads=[vcs[p][s].buf, vn[p][s].buf, eb[x].buf, ones_b.buf],
                                         writes=[od.buf])
                                    S.op("dve", lambda e, od=od, x=x: e.reciprocal(out=rd[x].h[:, 0:32],
                                                                                  in_=od.h[:, 128:160]),
                                         reads=[od.buf], writes=[rd[x].buf])
                                    S.op("dve", lambda e, od=od, x=x, h=h, s=s: e.tensor_tensor(
                                        out=A.h[:, h, 1024 + 32 * s:1056 + 32 * s], in0=od.h[:, 0:32],
                                        in1=rd[x].h[:, 0:32], op=ALU.mult),
                                         reads=[od.buf, rd[x].buf], writes=[A.sub[h]])
                        S.run()

                    with ExitStack() as es:
                        S = Sched(gs)
                        ring = [sb(es, f"g2_w{i}", [128, 4096], BF16) for i in range(6)]
                        xr = [sb(es, f"g2_xr{i}", [128, Tg], F32) for i in range(2)]
                        orow = [sb(es, f"g2_or{i}", [128, Tg], F32) for i in range(2)]
                        sq = [sb(es, f"g2_sq{i}", [128, Tg], F32) for i in range(2)]
                        tmp = sb(es, "g2_tmp", [128, Tg], F32)
                        hl = [(sb(es, f"g2_hi{i}", [128, Tg], BF16), sb(es, f"g2_lo{i}", [128, Tg], BF16)) for i in range(2)]
                        pss = [ps(es, f"g2_ps{i}", [128, 512]) for i in range(4)]
                        st = [ps(es, f"g2_st{i}", [128, 512]) for i in range(len(tl))]
                        cat3 = cat_d.rearrange("(k p) t -> p k t", p=128)
                        for hf in range(4):
                            S.dma("sp", A.h[:, 16 + 4 * hf:20 + 4 * hf, :], cat3[:, 4 * hf:4 * hf + 4, 0:Tg],
                                  reads=[DB("cat", i) for i in range(4 * hf, 4 * hf + 4)],
                                  writes=[A.sub[k] for k in range(16 + 4 * hf, 20 + 4 * hf)])

                        def epi(i, ti, c, n, pt):
                            if ti == 0:
                                x_ = xr[i % 2]
                                S.dma("sp", x_.h[:], x_src[i * 128:(i + 1) * 128, G0:G0 + Tg], reads=[DB("x", i)],
                                      writes=[x_.buf])
                            S.op("dve", lambda e: e.tensor_tensor(out=orow[i % 2].h[:, c:c + n], in0=pt.h[:, 0:n],
                                                                  in1=xr[i % 2].h[:, c:c + n], op=ALU.add),
                                 reads=[pt.buf, xr[i % 2].buf], writes=[orow[i % 2].buf])

                        def epi_end(i):
                            o_ = orow[i % 2]
                            s_ = sq[i % 2]
                            S.dma("sp", x_d[i * 128:(i + 1) * 128, G0:G0 + Tg], o_.h[:], reads=[o_.buf],
                                  writes=[DB("x", i)])
                            S.op("act", lambda e: e.activation(out=s_.h[:], in_=o_.h[:], func=AF.Square),
                                 reads=[o_.buf], writes=[s_.buf])
                            stat_row(S, st, tl, s_, hl[i % 2], i == 0, i == 31, defer_to=i + 1)

                        run_gemm(S, A.h, A.sub, 32, tl, 32, lambda i, p: w_out[l * 32 + i], ring, pss, epi, epi_end, 2048)
                        rstd_from_stats(S, st, tl, 0, tmp)
                        S.run()

                with ExitStack() as es:
                    S = Sched(gs)
                    A = sb(es, "g3_A", [128, 32, Tg], BF16, nsub=32)
                    ring = [sb(es, f"g3_w{i}", [128, 4096], BF16) for i in range(6)]
                    xr = [sb(es, f"g3_xr{i}", [128, Tg], F32) for i in range(2)]
                    stA = [sb(es, f"g3_st{i}", [128, Tg], F32) for i in range(2)]
                    o16 = [sb(es, f"g3_o{i}", [128, Tg], BF16) for i in range(3)]
                    pss = [ps(es, f"g3_ps{i}", [128, 512]) for i in range(6)]
                    for blk in range(32):
                        x_ = xr[blk % 2]
                        S.dma("sp", x_.h[:], x_d[blk * 128:(blk + 1) * 128, G0:G0 + Tg],
                              reads=[DB("x", blk)], writes=[x_.buf])
                        S.op("dve", lambda e, x_=x_, blk=blk: e.scalar_tensor_tensor(
                            out=A.h[:, blk, :], in0=x_.h[:], scalar=pcol("gffn", l * 32 + blk),
                            in1=rstd_b.h[:, 0:Tg], op0=ALU.mult, op1=ALU.mult),
                             reads=[x_.buf, rstd_b.buf], writes=[A.sub[blk]])

                    def epi(i, ti, c, n, pt):
                        f = i // 2
                        sa = stA[f % 2]
                        if i % 2 == 0:
                            S.op("act", lambda e: e.activation(out=sa.h[:, c:c + n], in_=pt.h[:, 0:n], func=AF.Silu),
                                 reads=[pt.buf], writes=[sa.buf])
                        else:
                            o_ = o16[f % 3]
                            S.op("dve", lambda e: e.tensor_tensor(out=o_.h[:, c:c + n], in0=pt.h[:, 0:n],
                                                                  in1=sa.h[:, c:c + n], op=ALU.mult),
                                 reads=[pt.buf, sa.buf], writes=[o_.buf])

                    def epi_end(i):
                        if i % 2 == 1:
                            f = i // 2
                            o_ = o16[f % 3]
                            S.dma("sp", hid_d[f * 128:(f + 1) * 128, 0:Tg], o_.h[:], reads=[o_.buf],
                                  writes=[DB("hid", f)])

                    run_gemm(S, A.h, A.sub, 32, tl, 172, lambda i, p: w_gu[l * 172 + i], ring, pss, epi, epi_end, 2048)
                    S.run()

                for sg in range(2):
                    c_lo = 512 * sg
                    Ts = 512 if sg == 0 else Tg - 512
                    tls = tiles_of(Ts)
                    with ExitStack() as es:
                        S = Sched(gs)
                        A = sb(es, "g4_A", [128, FC, Ts], BF16, nsub=22)
                        ring = [sb(es, f"g4_w{i}", [128, 5504], BF16) for i in range(5)]
                        xr = [sb(es, f"g4_xr{i}", [128, Ts], F32) for i in range(2)]
                        orow = [sb(es, f"g4_or{i}", [128, Ts], F32) for i in range(2)]
                        sq = [sb(es, f"g4_sq{i}", [128, Ts], F32) for i in range(2)]
                        tmp = sb(es, "g4_tmp", [128, Ts], F32)
                        hl = [(sb(es, f"g4_hi{i}", [128, Ts], BF16), sb(es, f"g4_lo{i}", [128, Ts], BF16)) for i in range(2)]
                        pss = [ps(es, f"g4_ps{i}", [128, 512]) for i in range(4)]
                        st = [ps(es, f"g4_st{i}", [128, 512]) for i in range(len(tls))]
                        hid3 = hid_d.rearrange("(k p) t -> p k t", p=128)
                        for hf in range(22):
                            k0, k1 = 4 * hf, min(FC, 4 * hf + 4)
                            S.dma("sp", A.h[:, k0:k1, :], hid3[:, k0:k1, c_lo:c_lo + Ts],
                                  reads=[DB("hid", f) for f in range(k0, k1)], writes=[A.sub[hf]])

                        def epi(i, ti, c, n, pt):
                            if ti == 0:
                                x_ = xr[i % 2]
                                S.dma("sp", x_.h[:], x_d[i * 128:(i + 1) * 128, G0 + c_lo:G0 + c_lo + Ts],
                                      reads=[DB("x", i)], writes=[x_.buf])
                            S.op("dve", lambda e: e.tensor_tensor(out=orow[i % 2].h[:, c:c + n], in0=pt.h[:, 0:n],
                                                                  in1=xr[i % 2].h[:, c:c + n], op=ALU.add),
                                 reads=[pt.buf, xr[i % 2].buf], writes=[orow[i % 2].buf])

                        def epi_end(i):
                            o_ = orow[i % 2]
                            s_ = sq[i % 2]
                            S.dma("sp", x_d[i * 128:(i + 1) * 128, G0 + c_lo:G0 + c_lo + Ts], o_.h[:], reads=[o_.buf],
                                  writes=[DB("x", i)])
                            S.op("act", lambda e: e.activation(out=s_.h[:], in_=o_.h[:], func=AF.Square),
                                 reads=[o_.buf], writes=[s_.buf])
                            stat_row(S, st, tls, s_, hl[i % 2], i == 0, i == 31, defer_to=i + 1)

                        run_gemm(S, A.h, A.sub, FC, tls, 32, lambda i, p: w_dn[l * 32 + i][:, p * 5504:(p + 1) * 5504], ring, pss, epi, epi_end, 1376, P=2)
                        rstd_from_stats(S, st, tls, c_lo, tmp)
                        S.run()

            with ExitStack() as es:
                S = Sched(gs)
                xr = [sb(es, f"fn_x{i}", [128, Tg], F32) for i in range(3)]
                orow = [sb(es, f"fn_o{i}", [128, Tg], F32) for i in range(3)]
                for blk in range(32):
                    x_ = xr[blk % 3]
                    o_ = orow[blk % 3]
                    S.dma("sp", x_.h[:], x_d[blk * 128:(blk + 1) * 128, G0:G0 + Tg], reads=[DB("x", blk)],
                          writes=[x_.buf])
                    S.op("dve", lambda e, x_=x_, o_=o_, blk=blk: e.scalar_tensor_tensor(
                        out=o_.h[:], in0=x_.h[:], scalar=pcol("gfin", blk), in1=rstd_b.h[:, 0:Tg],
                        op0=ALU.mult, op1=ALU.mult), reads=[x_.buf, rstd_b.buf], writes=[o_.buf])
                    S.dma("sp", yT[blk * 128:(blk + 1) * 128, G0:G0 + Tg], o_.h[:], reads=[o_.buf])
                if g == 1:
                    S.dma("sp", cbo_d, cbo_sb.h[:], reads=[cbo_sb.buf])
                    S.dma("sp", cco_d, cco_sb.h[:], reads=[cco_sb.buf])
                S.run()
    return nc


def _blockify(w, perm=None):
    K, N = w.shape
    a = w.reshape(K // 128, 128, N // 128, 128).transpose(2, 1, 0, 3)
    if perm is not None:
        a = a[perm]
    return np.ascontiguousarray(a).reshape(a.shape[0], 128, K)


def _pvec(v):
    return np.ascontiguousarray(v.reshape(-1, 128).T)


_CACHE = {}


def _prep(x_prompt, x_sample, cache_attn_k, cache_attn_v, cache_conv_b, cache_conv_c,
          norm_mix_g, w_in, rel_bias, conv_b_w, conv_c_w, conv_c_b, ln_c_g, ln_c_b,
          w_out, norm_ffn_g, w_ffn_gate, w_ffn_up, w_ffn_down, final_norm_g, cores=range(8)):
    f = lambda a: np.asarray(a, dtype=np.float32)
    x_prompt, x_sample = f(x_prompt), f(x_sample)
    cache_attn_k, cache_attn_v = f(cache_attn_k), f(cache_attn_v)
    cache_conv_b, cache_conv_c = f(cache_conv_b), f(cache_conv_c)
    rel_bias = f(rel_bias)
    n_cores = 8

    perm = [ORIG_BLK[k] + i for (k, i) in IN_ORDER]
    w_in_b = np.concatenate([_blockify(f(w_in[l]), perm) for l in range(L)], axis=0)
    w_out_b = np.concatenate([_blockify(f(w_out[l])) for l in range(L)], axis=0)
    gu = []
    for l in range(L):
        gb = _blockify(f(w_ffn_gate[l]))
        ub = _blockify(f(w_ffn_up[l]))
        gu.append(np.stack([gb, ub], axis=1).reshape(172, 128, 4096))
    w_gu_b = np.concatenate(gu, axis=0)
    w_dn_b = np.concatenate([_blockify(f(w_ffn_down[l])) for l in range(L)], axis=0)

    k = np.arange(128)[:, None, None]
    j = np.arange(5)[None, :, None]
    q = np.arange(128)[None, None, :]
    d = q - k + 128 * (4 - j)
    idx = np.clip(d, -256, 256) + 256
    masked = ((j == 4) & (k >= 64) & (q < 64)) | ((j == 0) & (k < 64) & (q >= 64))
    biasp = np.where(masked[None, None], np.float32(NEG), rel_bias[:, :, idx]).astype(np.float32)
    biasp = np.ascontiguousarray(biasp.reshape(L * 16, 128, 640))
    q32 = np.arange(32)[None, None, :]
    ds = np.where(j < 4, q32 + 512 - (128 * j + k), q32 - k)
    idxs = np.clip(ds, -256, 256) + 256
    biass = np.ascontiguousarray(rel_bias[:, :, idxs].astype(np.float32).reshape(L * 16, 128, 160))
    ident = np.eye(128, dtype=np.float32)

    def par_common():
        p = np.zeros((128, NPAR), np.float32)
        for l in range(L):
            p[:, OFF["gmix"] + l * 32: OFF["gmix"] + (l + 1) * 32] = _pvec(f(norm_mix_g[l]))
            p[:, OFF["gffn"] + l * 32: OFF["gffn"] + (l + 1) * 32] = _pvec(f(norm_ffn_g[l]))
            cb = f(conv_b_w[l]).reshape(3, 8, 128).transpose(2, 1, 0).reshape(128, 24)
            p[:, OFF["cbw"] + l * 24: OFF["cbw"] + (l + 1) * 24] = cb
            cc = f(conv_c_w[l]).reshape(31, 8, 128).transpose(2, 1, 0).reshape(128, 248)
            p[:, OFF["ccw"] + l * 248: OFF["ccw"] + (l + 1) * 248] = cc
            p[:, OFF["ccb"] + l * 8: OFF["ccb"] + (l + 1) * 8] = _pvec(f(conv_c_b[l]))
            p[:, OFF["lng"] + l * 8: OFF["lng"] + (l + 1) * 8] = _pvec(f(ln_c_g[l]))
            p[:, OFF["lnb"] + l * 8: OFF["lnb"] + (l + 1) * 8] = _pvec(f(ln_c_b[l]))
        p[:, OFF["gfin"]: OFF["gfin"] + 32] = _pvec(f(final_norm_g))
        return p

    pc = par_common()
    in_maps = []
    for c in cores:
        xTc = np.ascontiguousarray(np.concatenate(
            [x_prompt[c].T, x_sample[2 * c].T, x_sample[2 * c + 1].T], axis=1))
        p = pc.copy()
        hb = cache_conv_b[:, 2 * c:2 * c + 2].reshape(L, 2, 2, 8, 128).transpose(4, 0, 1, 3, 2).reshape(128, -1)
        hc = cache_conv_c[:, 2 * c:2 * c + 2].reshape(L, 2, 30, 8, 128).transpose(4, 0, 1, 3, 2).reshape(128, -1)
        p[:, OFF["histb"]:OFF["histb"] + hb.shape[1]] = hb
        p[:, OFF["histc"]:OFF["histc"] + hc.shape[1]] = hc
        ck = cache_attn_k[:, 2 * c:2 * c + 2]
        ckT = np.ascontiguousarray(ck.transpose(0, 1, 3, 4, 2)).reshape(L * 2 * 16, 128, 512)
        cv = np.ascontiguousarray(cache_attn_v[:, 2 * c:2 * c + 2]).reshape(L * 2, 512, 16, 128)
        in_maps.append({"xT": xTc, "w_in": w_in_b, "w_out": w_out_b, "w_gu": w_gu_b, "w_dn": w_dn_b, "par": p,
                        "biasp": biasp, "biass": biass, "ckT": ckT, "cv": cv, "ident": ident})

    return in_maps


def kernel(**inputs):
    n_cores = 8
    in_maps = _prep(**inputs)
    if "nc" not in _CACHE:
        _CACHE["nc"] = build_program()
    nc = _CACHE["nc"]
    res = run_bass_kernel_spmd(nc, in_maps, core_ids=list(range(n_cores)))
    R = res.results

    y_prompt = np.empty((8, 2048, D), np.float32)
    y_sample = np.empty((16, 32, D), np.float32)
    pk = np.empty((L, 8, 512, 16, 128), np.float32)
    pv = np.empty_like(pk)
    sk = np.empty((L, 16, 32, 16, 128), np.float32)
    sv = np.empty_like(sk)
    pb = np.empty((L, 8, 2, 1024), np.float32)
    pcv = np.empty((L, 8, 30, 1024), np.float32)
    sbo = np.empty((L, 16, 2, 1024), np.float32)
    sco = np.empty((L, 16, 30, 1024), np.float32)
    for c in range(n_cores):
        r = R[c]
        yT = r["yT"]
        y_prompt[c] = yT[:, :2048].T
        for s in range(2):
            y_sample[2 * c + s] = yT[:, 2048 + 32 * s:2080 + 32 * s].T
        for name, P_, S_ in (("kTo", pk, sk), ("vTo", pv, sv)):
            a = r[name].reshape(L, 2048, 576)
            for l in range(L):
                P_[l, c] = a[l][:, :512].T.reshape(512, 16, 128)
                for s in range(2):
                    S_[l, 2 * c + s] = a[l][:, 512 + 32 * s:544 + 32 * s].T.reshape(32, 16, 128)
        cb = r["cbo"].reshape(128, L, 8, 3, 2)
        cc = r["cco"].reshape(128, L, 8, 3, 30)
        for l in range(L):
            pb[l, c] = cb[:, l, :, 0, :].transpose(2, 1, 0).reshape(2, 1024)
            pcv[l, c] = cc[:, l, :, 0, :].transpose(2, 1, 0).reshape(30, 1024)
            for s in range(2):
                sbo[l, 2 * c + s] = cb[:, l, :, 1 + s, :].transpose(2, 1, 0).reshape(2, 1024)
                sco[l, 2 * c + s] = cc[:, l, :, 1 + s, :].transpose(2, 1, 0).reshape(30, 1024)
    return (y_prompt, y_sample, pk, pv, pb, pcv, sk, sv, sbo, sco)
```

```python
import numpy as np
from contextlib import ExitStack
import concourse.bass as bass
import concourse.mybir as mybir
from concourse.bass_utils import run_bass_kernel_spmd

F32 = mybir.dt.float32
BF16 = mybir.dt.bfloat16
AF = mybir.ActivationFunctionType
ALU = mybir.AluOpType

L = 2
D = 4096
KC = 32
DFF = 11008
FC = 86
H = 16
TOK = 2112
TG = (1024, 1088)
EPS = 1e-6
SCALE = 128 ** -0.5
NEG = -30000.0
_STOP = None
_NB1 = None
_NOSTAT = False

OFF = {}
_o = 0
for _n, _sz in [("gmix", L * 32), ("gffn", L * 32), ("gfin", 32), ("cbw", L * 8 * 3), ("ccw", L * 8 * 31),
                ("ccb", L * 8), ("lng", L * 8), ("lnb", L * 8), ("histb", L * 2 * 8 * 2),
                ("histc", L * 2 * 8 * 30)]:
    OFF[_n] = _o
    _o += _sz
NPAR = _o

_QKV = [("q", h) for h in range(16)] + [("k", h) for h in range(16)] + [("v", h) for h in range(16)]
IN_ORDER = []
for _i in range(8):
    IN_ORDER += [("cg", _i), ("ca", _i)] + _QKV[6 * _i:6 * _i + 6] + [("sc", _i), ("sh", _i), ("sb", _i)]
ORIG_BLK = {"q": 0, "k": 16, "v": 32, "sb": 48, "sc": 56, "sh": 64, "ca": 72, "cg": 80}


class Tok:
    __slots__ = ("eng", "sem", "value", "marked", "is_dma")

    def __init__(self, eng, sem=None, value=None, is_dma=False):
        self.eng = eng
        self.sem = sem
        self.value = value
        self.marked = is_dma
        self.is_dma = is_dma


class Buf:
    __slots__ = ("w", "r")

    def __init__(self):
        self.w = None
        self.r = []


class GS:
    NPOOL = 20

    def __init__(self, nc, es):
        self.nc = nc
        self.eng_sem = {e: es.enter_context(nc.semaphore("sem_" + e)) for e in ("pe", "act", "dve", "pool")}
        self.cnt = {e: 0 for e in self.eng_sem}
        self.dma_sems = {q: [es.enter_context(nc.semaphore(f"dq_{q}{i}")) for i in range(self.NPOOL)]
                         for q in ("sp", "pool", "act")}
        self.dma_cnt = {q: [0] * self.NPOOL for q in self.dma_sems}
        self.dma_rr = {q: 0 for q in self.dma_sems}
        self.known = {e: {} for e in ("pe", "act", "dve", "pool", "sp")}
        self.bufs = []
        self.phase = 0

    def buf(self):
        b = Buf()
        self.bufs.append(b)
        return b


class Sched:
    ENGS = ("pe", "act", "dve", "pool", "sp")
    BLK = {"pe": "tensor", "act": "scalar", "dve": "vector", "pool": "gpsimd", "sp": "sync"}

    def __init__(self, gs):
        self.gs = gs
        self.nc = gs.nc
        self.ops = {e: [] for e in self.ENGS}
        self.used_dma = {e: {} for e in self.ENGS}

    def _deps(self, eng, reads, writes):
        deps = []
        for b in reads:
            if b.w is not None:
                deps.append(b.w)
        for b in writes:
            if b.w is not None:
                deps.append(b.w)
            deps.extend(b.r)
        out = []
        for d in deps:
            if (not d.is_dma) and d.eng == "pe" and eng == "pe":
                continue
            d.marked = True
            out.append(d)
        return out

    @staticmethod
    def _post(tok, reads, writes):
        for b in reads:
            b.r.append(tok)
        for b in writes:
            b.w = tok
            b.r = []

    def op(self, eng, emit, reads=(), writes=()):
        deps = self._deps(eng, reads, writes)
        tok = Tok(eng, self.gs.eng_sem[eng])
        self.ops[eng].append((deps, emit, tok))
        self._post(tok, reads, writes)
        return tok

    def dma(self, q, out, in_, reads=(), writes=()):
        gs = self.gs
        deps = self._deps(q, reads, writes)
        i = gs.dma_rr[q]
        gs.dma_rr[q] = (i + 1) % gs.NPOOL
        sem = gs.dma_sems[q][i]
        prev = gs.dma_cnt[q][i]
        gs.dma_cnt[q][i] = prev + 16
        tok = Tok(q, sem, prev + 16, is_dma=True)
        if prev > 0:
            deps.append(Tok(q, sem, prev, is_dma=True))
        self.used_dma[q][i] = (sem, prev + 16)
        self.ops[q].append((deps, (lambda e, o=out, s=in_: e.dma_start(out=o, in_=s)), tok))
        self._post(tok, reads, writes)
        return tok

    def run(self):
        gs = self.gs
        gs.phase += 1
        if _STOP is not None and gs.phase > _STOP:
            for b in gs.bufs:
                b.w = None
                b.r = []
            return
        for e in ("pe", "act", "dve", "pool"):
            for deps, emit, tok in self.ops[e]:
                if (not tok.is_dma) and tok.marked:
                    gs.cnt[e] += 1
                    tok.value = gs.cnt[e]
        with self.nc.Block() as blk:
            for e in self.ENGS:
                if not self.ops[e]:
                    continue

                def body(eh, e=e):
                    known = gs.known[e]
                    for deps, emit, tok in self.ops[e]:
                        need = {}
                        for d in deps:
                            k = id(d.sem)
                            if k not in need or need[k][1] < d.value:
                                need[k] = (d.sem, d.value)
                        for k, (sh, v) in need.items():
                            if known.get(k, 0) < v:
                                eh.wait_ge(sh, v)
                                known[k] = v
                        ins = emit(eh)
                        if tok.is_dma:
                            ins.then_inc(tok.sem, 16)
                        elif tok.marked:
                            ins.then_inc(tok.sem, 1)
                    for i, (sh, v) in self.used_dma[e].items():
                        k = id(sh)
                        if known.get(k, 0) < v:
                            eh.wait_ge(sh, v)
                            known[k] = v

                getattr(blk, self.BLK[e])(body)
        for b in gs.bufs:
            b.w = None
            b.r = []


class T:
    def __init__(self, gs, h, nsub=0):
        self.h = h
        self.buf = gs.buf()
        self.sub = [gs.buf() for _ in range(nsub)]


def tiles_of(n):
    out = []
    c = 0
    while c < n:
        w = min(512, n - c)
        out.append((c, w))
        c += w
    return out


def build_program():
    nc = bass.Bass("TRN2", target_bir_lowering=False)
    di = lambda n, s, dt=F32: nc.dram_tensor(n, s, dt, kind="ExternalInput").ap()
    do = lambda n, s, dt=F32: nc.dram_tensor(n, s, dt, kind="ExternalOutput").ap()
    dx = lambda n, s, dt: nc.dram_tensor(n, s, dt).ap()

    xT = di("xT", [D, TOK])
    w_in = di("w_in", [L * 88, 128, 4096])
    w_out = di("w_out", [L * 32, 128, 4096])
    w_gu = di("w_gu", [L * 172, 128, 4096])
    w_dn = di("w_dn", [L * 32, 128, DFF])
    par_d = di("par", [128, NPAR])
    biasp_d = di("biasp", [L * 16, 128, 640])
    biass_d = di("biass", [L * 16, 128, 160])
    ckT_d = di("ckT", [L * 2 * 16, 128, 512])
    cv_d = di("cv", [L * 2, 512, 16, 128])
    ident_d = di("ident", [128, 128])

    yT = do("yT", [D, TOK])
    kTo = do("kTo", [L * 2048, 576])
    vTo = do("vTo", [L * 2048, 576])
    cbo_d = do("cbo", [128, L * 8 * 6])
    cco_d = do("cco", [128, L * 8 * 90])

    x_d = dx("x_d", [D, TOK], F32)
    q_d = dx("q_d", [16 * 128, 1088], BF16)
    k_d = dx("k_d", [L * 16 * 128, TOK], BF16)
    v_d = dx("v_d", [L * 16 * 128, TOK], BF16)
    cat_d = dx("cat_d", [2048, 1088], BF16)
    zc_d = dx("zc_d", [1024, 1088], F32)
    hid_d = dx("hid_d", [DFF, 1088], BF16)

    with ExitStack() as top:
        gs = GS(nc, top)
        uid = [0]

        def uname(n):
            uid[0] += 1
            return f"{n}_u{uid[0]}"

        sb = lambda es, n, s, dt, nsub=0: T(gs, es.enter_context(nc.sbuf_tensor(uname(n), s, dt)), nsub)
        ps = lambda es, n, s, dt=F32: T(gs, es.enter_context(nc.psum_tensor(uname(n), s, dt)))

        par = sb(top, "par", [128, NPAR], F32)
        ones_b = sb(top, "ones_b", [128, 128], BF16)
        ident_b = sb(top, "ident_b", [128, 128], BF16)
        rstd_b = sb(top, "rstd_b", [128, 1088], F32)
        mean_b = sb(top, "mean_b", [128, 1088], F32)
        halo_b = sb(top, "halo_b", [128, L * 8 * 2], F32)
        halo_c = sb(top, "halo_c", [128, L * 8 * 30], F32)
        cbo_sb = sb(top, "cbo_sb", [128, L * 8 * 6], F32)
        cco_sb = sb(top, "cco_sb", [128, L * 8 * 90], F32)

        dbuf = {}

        def DB(*key):
            if key not in dbuf:
                dbuf[key] = gs.buf()
            return dbuf[key]

        pcol = lambda name, idx: par.h[:, OFF[name] + idx: OFF[name] + idx + 1]

        S = Sched(gs)
        S.dma("sp", par.h[:], par_d, writes=[par.buf])
        S.dma("pool", ident_b.h[:], ident_d, writes=[ident_b.buf])
        S.op("dve", lambda e: e.memset(ones_b.h[:], 1.0), writes=[ones_b.buf])
        S.run()

        def rstd_from_stats(S, stat_tiles, tl, col0, tmp):
            for (c, n), st in zip(tl, stat_tiles):
                S.op("dve", lambda e, c=c, n=n, st=st: e.tensor_scalar(
                    out=tmp.h[:, c:c + n], in0=st.h[:, 0:n], scalar1=1.0 / D, scalar2=EPS,
                    op0=ALU.mult, op1=ALU.add), reads=[st.buf], writes=[tmp.buf])
                S.op("act", lambda e, c=c, n=n: e.activation(out=tmp.h[:, c:c + n], in_=tmp.h[:, c:c + n],
                                                             func=AF.Sqrt), reads=[tmp.buf], writes=[tmp.buf])
                S.op("dve", lambda e, c=c, n=n: e.reciprocal(out=rstd_b.h[:, col0 + c: col0 + c + n],
                                                             in_=tmp.h[:, c:c + n]),
                     reads=[tmp.buf], writes=[rstd_b.buf])

        def stat_row(S, sts, tl_, src, hl, first, last, defer_to=None):
            if _NOSTAT:
                return
            hi, lo = hl
            S.op("act", lambda e: e.activation(out=hi.h[:], in_=src.h[:], func=AF.Copy),
                 reads=[src.buf], writes=[hi.buf])
            S.op("dve", lambda e: e.tensor_tensor(out=lo.h[:], in0=src.h[:], in1=hi.h[:], op=ALU.subtract),
                 reads=[src.buf, hi.buf], writes=[lo.buf])
            def pe_part():
                for (c, n), st_ in zip(tl_, sts):
                    def emit(e, c=c, n=n, st_=st_):
                        e.matmul(st_.h[:, 0:n], lhsT=ones_b.h[:], rhs=hi.h[:, c:c + n], start=first, stop=False)
                        return e.matmul(st_.h[:, 0:n], lhsT=ones_b.h[:], rhs=lo.h[:, c:c + n], start=False, stop=last)
                    S.op("pe", emit, reads=[hi.buf, lo.buf, ones_b.buf], writes=[st_.buf])

            if defer_to is None:
                pe_part()
            else:
                S.deferred.append((defer_to, pe_part))

        def run_gemm(S, A3, a_bufs, kcn, tl, nblocks, wsrc, ring, pss, epi, epi_end, esz, P=1):
            R = len(ring)
            kp = kcn // P
            total = nblocks * P
            S.deferred = []
            nl = [0]

            def load(j):
                slot = ring[j % R]
                S.dma("pool", slot.h[:, 0:kp * 128].rearrange("p (c e) -> p c e", e=esz),
                      wsrc(j // P, j % P).rearrange("p (c e) -> p c e", e=esz), writes=[slot.buf])

            def flush(i):
                keep = []
                for (tgt, th) in S.deferred:
                    if tgt <= i:
                        th()
                    else:
                        keep.append((tgt, th))
                S.deferred = keep

            pi = 0
            for i in range(nblocks):
                while nl[0] < total and nl[0] < i * P + R:
                    load(nl[0])
                    nl[0] += 1
                slots = [ring[(i * P + p) % R] for p in range(P)]
                w3s = [sl.h[:, 0:kp * 128].rearrange("p (k n) -> p k n", n=128) for sl in slots]
                for ti, (c, n) in enumerate(tl):
                    pt = pss[pi % len(pss)]
                    pi += 1

                    def emit(e, w3s=w3s, pt=pt, c=c, n=n):
                        for kc in range(kcn):
                            ins = e.matmul(pt.h[:, 0:n], lhsT=w3s[kc // kp][:, kc % kp, :], rhs=A3[:, kc, c:c + n],
                                           start=(kc == 0), stop=(kc == kcn - 1))
                        return ins

                    S.op("pe", emit, reads=list(a_bufs) + [sl.buf for sl in slots], writes=[pt.buf])
                    epi(i, ti, c, n, pt)
                flush(i)
                epi_end(i)
            flush(10 ** 9)

        for g in range(2):
            Tg = TG[g]
            tl = tiles_of(Tg)
            G0 = 1024 * g

            with ExitStack() as es:
                S = Sched(gs)
                xr = [sb(es, f"s0_xr{i}", [128, Tg], F32) for i in range(3)]
                sq = [sb(es, f"s0_sq{i}", [128, Tg], F32) for i in range(2)]
                tmp = sb(es, "s0_tmp", [128, Tg], F32)
                hl = [(sb(es, f"s0_hi{i}", [128, Tg], BF16), sb(es, f"s0_lo{i}", [128, Tg], BF16)) for i in range(2)]
                st = [ps(es, f"s0_st{i}", [128, 512]) for i in range(len(tl))]
                for blk in range(32):
                    x_ = xr[blk % 3]
                    s_ = sq[blk % 2]
                    S.dma("sp", x_.h[:], xT[blk * 128:(blk + 1) * 128, G0:G0 + Tg], writes=[x_.buf])
                    S.op("act", lambda e, x_=x_, s_=s_: e.activation(out=s_.h[:], in_=x_.h[:], func=AF.Square),
                         reads=[x_.buf], writes=[s_.buf])
                    stat_row(S, st, tl, s_, hl[blk % 2], blk == 0, blk == 31)
                rstd_from_stats(S, st, tl, 0, tmp)
                S.run()

            for l in range(L):
                x_src = xT if l == 0 else x_d

                with ExitStack() as es:
                    S = Sched(gs)
                    A = sb(es, "g1_A", [128, 32, Tg], BF16, nsub=32)
                    ring = [sb(es, f"g1_w{i}", [128, 4096], BF16) for i in range(5)]
                    hl = [(sb(es, f"g1_hi{i}", [128, Tg], BF16), sb(es, f"g1_lo{i}", [128, Tg], BF16)) for i in range(2)]
                    xr = [sb(es, f"g1_xr{i}", [128, Tg], F32) for i in range(4)]
                    stA = sb(es, "g1_stA", [128, Tg], F32)
                    UL = 1094 if g == 1 else 1026
                    CL = 1178 if g == 1 else 1054
                    ubuf = sb(es, "g1_ubuf", [128, UL], F32)
                    cbuf = sb(es, "g1_cbuf", [128, CL], F32)
                    acc = [sb(es, f"g1_acc{i}", [128, CL], F32) for i in range(2)]
                    zcrow = [sb(es, f"g1_zc{i}", [128, Tg], F32) for i in range(2)]
                    o16 = [sb(es, f"g1_o16{i}", [128, Tg], BF16) for i in range(3)]
                    okv = [sb(es, f"g1_okv{i}", [128, 576], F32) for i in range(2)]
                    pss = [ps(es, f"g1_ps{i}", [128, 512]) for i in range(4)]
                    st = [ps(es, f"g1_st{i}", [128, 512]) for i in range(len(tl))]

                    for blk in range(32):
                        x_ = xr[blk % 4]
                        S.dma("sp" if blk % 2 == 0 else "act", x_.h[:], x_src[blk * 128:(blk + 1) * 128, G0:G0 + Tg],
                              reads=[DB("x", blk)], writes=[x_.buf])
                        S.op("dve", lambda e, x_=x_, blk=blk: e.scalar_tensor_tensor(
                            out=A.h[:, blk, :], in0=x_.h[:], scalar=pcol("gmix", l * 32 + blk),
                            in1=rstd_b.h[:, 0:Tg], op0=ALU.mult, op1=ALU.mult),
                             reads=[x_.buf, rstd_b.buf], writes=[A.sub[blk]])

                    if g == 0:
                        S.op("dve", lambda e: e.memset(ubuf.h[:, 0:2], 0.0), writes=[ubuf.buf])
                        S.op("dve", lambda e: e.memset(cbuf.h[:, 0:30], 0.0), writes=[cbuf.buf])
                    state = {"o16": 0, "okv": 0, "acc": 0, "zc": 0, "nC": 0}

                    def seg_map(c, n, upos, spos):
                        if c < 1024:
                            return [(0, n, upos + c)]
                        return [(0, 32, spos[0]), (32, 64, spos[1])]

                    def epi(i, ti, c, n, pt):
                        kind, idx = IN_ORDER[i]
                        if kind in ("q", "k", "v"):
                            o_ = o16[state["o16"] % 3]
                            S.op("act", lambda e: e.activation(out=o_.h[:, c:c + n], in_=pt.h[:, 0:n], func=AF.Copy),
                                 reads=[pt.buf], writes=[o_.buf])
                            if kind != "q" and g == 1 and ti >= 1:
                                ok = okv[state["okv"] % 2]
                                oc = 0 if ti == 1 else 512
                                S.op("act", lambda e: e.activation(out=ok.h[:, oc:oc + n], in_=pt.h[:, 0:n], func=AF.Copy),
                                     reads=[pt.buf], writes=[ok.buf])
                        elif kind == "sc":
                            S.op("act", lambda e: e.activation(out=stA.h[:, c:c + n], in_=pt.h[:, 0:n], func=AF.Copy),
                                 reads=[pt.buf], writes=[stA.buf])
                        elif kind == "sh":
                            for (a, b, dst) in seg_map(c, n, 2, (1028, 1062)):
                                S.op("dve", lambda e, a=a, b=b, dst=dst: e.tensor_tensor(
                                    out=ubuf.h[:, dst:dst + (b - a)], in0=pt.h[:, a:b], in1=stA.h[:, c + a:c + b],
                                    op=ALU.mult), reads=[pt.buf, stA.buf], writes=[ubuf.buf])
                        elif kind == "sb":
                            o_ = o16[state["o16"] % 3]
                            ac = acc[state["acc"] % 2]
                            for (a, b, src) in seg_map(c, n, 0, (1026, 1060)):
                                S.op("dve", lambda e, a=a, b=b, src=src: e.tensor_tensor(
                                    out=o_.h[:, c + a:c + b], in0=pt.h[:, a:b], in1=ac.h[:, src:src + (b - a)],
                                    op=ALU.mult), reads=[pt.buf, ac.buf], writes=[o_.buf])
                        elif kind == "cg":
                            S.op("act", lambda e: e.activation(out=stA.h[:, c:c + n], in_=pt.h[:, 0:n],
                                                               func=AF.Sigmoid),
                                 reads=[pt.buf], writes=[stA.buf])
                        elif kind == "ca":
                            for (a, b, dst) in seg_map(c, n, 30, (1084, 1146)):
                                S.op("dve", lambda e, a=a, b=b, dst=dst: e.tensor_tensor(
                                    out=cbuf.h[:, dst:dst + (b - a)], in0=pt.h[:, a:b], in1=stA.h[:, c + a:c + b],
                                    op=ALU.mult), reads=[pt.buf, stA.buf], writes=[cbuf.buf])

                    def epi_end(i):
                        kind, idx = IN_ORDER[i]
                        if kind in ("q", "k", "v"):
                            o_ = o16[state["o16"] % 3]
                            state["o16"] += 1
                            if kind == "q":
                                S.dma("sp", q_d[idx * 128:(idx + 1) * 128, 0:Tg], o_.h[:], reads=[o_.buf],
                                      writes=[DB("q", idx)])
                            else:
                                dst = k_d if kind == "k" else v_d
                                r0 = (l * 16 + idx) * 128
                                S.dma("sp", dst[r0:r0 + 128, G0:G0 + Tg], o_.h[:], reads=[o_.buf],
                                      writes=[DB(kind, l, idx)])
                                if g == 1:
                                    ok = okv[state["okv"] % 2]
                                    state["okv"] += 1
                                    od = kTo if kind == "k" else vTo
                                    r1 = l * 2048 + idx * 128
                                    S.dma("sp", od[r1:r1 + 128, :], ok.h[:], reads=[ok.buf])
                        elif kind == "sh":
                            if g == 1:
                                hb = OFF["histb"] + (l * 2 * 8 + idx) * 2
                                S.op("dve", lambda e: e.tensor_copy(out=ubuf.h[:, 0:2],
                                                                    in_=halo_b.h[:, (l * 8 + idx) * 2:(l * 8 + idx) * 2 + 2]),
                                     reads=[halo_b.buf], writes=[ubuf.buf])
                                for s in range(2):
                                    hb = OFF["histb"] + ((l * 2 + s) * 8 + idx) * 2
                                    S.op("dve", lambda e, s=s, hb=hb: e.tensor_copy(
                                        out=ubuf.h[:, 1026 + 34 * s:1028 + 34 * s], in_=par.h[:, hb:hb + 2]),
                                         writes=[ubuf.buf])
                            ac = acc[state["acc"] % 2]
                            n_out = UL - 2
                            w = lambda j: pcol("cbw", (l * 8 + idx) * 3 + j)
                            S.op("dve", lambda e: e.tensor_scalar(out=ac.h[:, 0:n_out], in0=ubuf.h[:, 0:n_out],
                                                                  scalar1=w(0), scalar2=None, op0=ALU.mult),
                                 reads=[ubuf.buf], writes=[ac.buf])
                            for j in (1, 2):
                                S.op("dve", lambda e, j=j: e.scalar_tensor_tensor(
                                    out=ac.h[:, 0:n_out], in0=ubuf.h[:, j:j + n_out], scalar=w(j),
                                    in1=ac.h[:, 0:n_out], op0=ALU.mult, op1=ALU.add),
                                     reads=[ubuf.buf, ac.buf], writes=[ac.buf])
                            if g == 0:
                                S.op("dve", lambda e: e.tensor_copy(
                                    out=halo_b.h[:, (l * 8 + idx) * 2:(l * 8 + idx) * 2 + 2], in_=ubuf.h[:, 1024:1026]),
                                     reads=[ubuf.buf], writes=[halo_b.buf])
                            else:
                                for s3, src in enumerate((1024, 1058, 1092)):
                                    o0 = (l * 8 + idx) * 6 + s3 * 2
                                    S.op("dve", lambda e, o0=o0, src=src: e.tensor_copy(
                                        out=cbo_sb.h[:, o0:o0 + 2], in_=ubuf.h[:, src:src + 2]),
                                         reads=[ubuf.buf], writes=[cbo_sb.buf])
                        elif kind == "sb":
                            o_ = o16[state["o16"] % 3]
                            state["o16"] += 1
                            state["acc"] += 1
                            S.dma("sp", cat_d[idx * 128:(idx + 1) * 128, 0:Tg], o_.h[:], reads=[o_.buf],
                                  writes=[DB("cat", idx)])
                        elif kind == "ca":
                            if g == 1:
                                S.op("dve", lambda e: e.tensor_copy(
                                    out=cbuf.h[:, 0:30], in_=halo_c.h[:, (l * 8 + idx) * 30:(l * 8 + idx) * 30 + 30]),
                                     reads=[halo_c.buf], writes=[cbuf.buf])
                                for s in range(2):
                                    hc = OFF["histc"] + ((l * 2 + s) * 8 + idx) * 30
                                    S.op("dve", lambda e, s=s, hc=hc: e.tensor_copy(
                                        out=cbuf.h[:, 1054 + 62 * s:1084 + 62 * s], in_=par.h[:, hc:hc + 30]),
                                         writes=[cbuf.buf])
                            ac = acc[state["acc"] % 2]
                            state["acc"] += 1
                            n_out = CL - 30
                            w = lambda j: pcol("ccw", (l * 8 + idx) * 31 + j)
                            S.op("dve", lambda e: e.tensor_scalar(
                                out=ac.h[:, 0:n_out], in0=cbuf.h[:, 0:n_out], scalar1=w(0),
                                scalar2=pcol("ccb", l * 8 + idx), op0=ALU.mult, op1=ALU.add),
                                 reads=[cbuf.buf], writes=[ac.buf])
                            for j in range(1, 31):
                                S.op("dve", lambda e, j=j: e.scalar_tensor_tensor(
                                    out=ac.h[:, 0:n_out], in0=cbuf.h[:, j:j + n_out], scalar=w(j),
                                    in1=ac.h[:, 0:n_out], op0=ALU.mult, op1=ALU.add),
                                     reads=[cbuf.buf, ac.buf], writes=[ac.buf])
                            if g == 0:
                                S.op("dve", lambda e: e.tensor_copy(
                                    out=halo_c.h[:, (l * 8 + idx) * 30:(l * 8 + idx) * 30 + 30],
                                    in_=cbuf.h[:, 1024:1054]), reads=[cbuf.buf], writes=[halo_c.buf])
                            else:
                                for s3, src in enumerate((1024, 1086, 1148)):
                                    o0 = (l * 8 + idx) * 90 + s3 * 30
                                    S.op("dve", lambda e, o0=o0, src=src: e.tensor_copy(
                                        out=cco_sb.h[:, o0:o0 + 30], in_=cbuf.h[:, src:src + 30]),
                                         reads=[cbuf.buf], writes=[cco_sb.buf])
                            zr = zcrow[state["zc"] % 2]
                            state["zc"] += 1
                            segs = [(0, 1024, 0)] + ([(1024, 32, 1054), (1056, 32, 1116)] if g == 1 else [])
                            for (dst, n_, src) in segs:
                                S.op("act", lambda e, dst=dst, n_=n_, src=src: e.activation(
                                    out=zr.h[:, dst:dst + n_], in_=ac.h[:, src:src + n_], func=AF.Copy),
                                     reads=[ac.buf], writes=[zr.buf])
                            stat_row(S, st, tl, zr, hl[state["nC"] % 2], state["nC"] == 0, state["nC"] == 7, defer_to=i + 4)
                            state["nC"] += 1
                            S.dma("sp", zc_d[idx * 128:(idx + 1) * 128, 0:Tg], zr.h[:], reads=[zr.buf],
                                  writes=[DB("zc", idx)])

                    run_gemm(S, A.h, A.sub, 32, tl, 88 if _NB1 is None else _NB1, lambda i, p: w_in[l * 88 + i], ring, pss, epi, epi_end, 2048)
                    for (c, n), stt in zip(tl, st):
                        S.op("act", lambda e, c=c, n=n, stt=stt: e.activation(
                            out=mean_b.h[:, c:c + n], in_=stt.h[:, 0:n], func=AF.Copy, scale=1.0 / 1024),
                             reads=[stt.buf], writes=[mean_b.buf])
                    S.run()

                with ExitStack() as es:
                    S = Sched(gs)
                    dd = [sb(es, f"ln_d{i}", [128, Tg], F32) for i in range(8)]
                    xr = [sb(es, f"ln_x{i}", [128, Tg], F32) for i in range(2)]
                    sq = [sb(es, f"ln_sq{i}", [128, Tg], F32) for i in range(2)]
                    tmp = sb(es, "ln_tmp", [128, Tg], F32)
                    hl = [(sb(es, f"ln_hi{i}", [128, Tg], BF16), sb(es, f"ln_lo{i}", [128, Tg], BF16)) for i in range(2)]
                    rs = sb(es, "ln_rs", [128, Tg], F32)
                    t1 = [sb(es, f"ln_t{i}", [128, Tg], F32) for i in range(2)]
                    o16 = [sb(es, f"ln_o{i}", [128, Tg], BF16) for i in range(2)]
                    st = [ps(es, f"ln_st{i}", [128, 512]) for i in range(len(tl))]
                    for blk in range(8):
                        x_ = xr[blk % 2]
                        s_ = sq[blk % 2]
                        d_ = dd[blk]
                        S.dma("sp", x_.h[:], zc_d[blk * 128:(blk + 1) * 128, 0:Tg], reads=[DB("zc", blk)],
                              writes=[x_.buf])
                        S.op("dve", lambda e, x_=x_, d_=d_: e.tensor_tensor(out=d_.h[:], in0=x_.h[:],
                                                                           in1=mean_b.h[:, 0:Tg], op=ALU.subtract),
                             reads=[x_.buf, mean_b.buf], writes=[d_.buf])
                        S.op("act", lambda e, s_=s_, d_=d_: e.activation(out=s_.h[:], in_=d_.h[:], func=AF.Square),
                             reads=[d_.buf], writes=[s_.buf])
                        stat_row(S, st, tl, s_, hl[blk % 2], blk == 0, blk == 7)
                    for (c, n), stt in zip(tl, st):
                        S.op("dve", lambda e, c=c, n=n, stt=stt: e.tensor_scalar(
                            out=tmp.h[:, c:c + n], in0=stt.h[:, 0:n], scalar1=1.0 / 1024, scalar2=EPS,
                            op0=ALU.mult, op1=ALU.add), reads=[stt.buf], writes=[tmp.buf])
                    S.op("act", lambda e: e.activation(out=tmp.h[:], in_=tmp.h[:], func=AF.Sqrt),
                         reads=[tmp.buf], writes=[tmp.buf])
                    S.op("dve", lambda e: e.reciprocal(out=rs.h[:], in_=tmp.h[:]), reads=[tmp.buf], writes=[rs.buf])
                    for blk in range(8):
                        d_ = dd[blk]
                        t_ = t1[blk % 2]
                        o_ = o16[blk % 2]
                        S.op("dve", lambda e, d_=d_, t_=t_: e.tensor_tensor(out=t_.h[:], in0=d_.h[:], in1=rs.h[:],
                                                                           op=ALU.mult),
                             reads=[d_.buf, rs.buf], writes=[t_.buf])
                        S.op("dve", lambda e, t_=t_, blk=blk: e.tensor_scalar(
                            out=t_.h[:], in0=t_.h[:], scalar1=pcol("lng", l * 8 + blk), scalar2=pcol("lnb", l * 8 + blk),
                            op0=ALU.mult, op1=ALU.add), reads=[t_.buf], writes=[t_.buf])
                        S.op("act", lambda e, t_=t_, o_=o_: e.activation(out=o_.h[:], in_=t_.h[:], func=AF.Silu),
                             reads=[t_.buf], writes=[o_.buf])
                        S.dma("sp", cat_d[(8 + blk) * 128:(9 + blk) * 128, 0:Tg], o_.h[:], reads=[o_.buf],
                              writes=[DB("cat", 8 + blk)])
                    S.run()

                with ExitStack() as es2:
                    A = sb(es2, "g2_A", [128, 32, Tg], BF16, nsub=32)
                    with ExitStack() as es:
                        S = Sched(gs)
                        KW = 1024 if g == 0 else 1600
                        C0 = 0 if g == 0 else 512
                        NT = KW // 128 if g == 0 else 12
                        qh = [sb(es, f"at_q{i}", [128, Tg], BF16) for i in range(2)]
                        kh = [sb(es, f"at_k{i}", [128, KW], BF16) for i in range(2)]
                        vh = [sb(es, f"at_v{i}", [128, KW], BF16) for i in range(2)]
                        vt = [sb(es, f"at_vt{i}", [128, NT, 128], BF16) for i in range(2)]
                        bp = [sb(es, f"at_bp{i}", [128, 640], F32) for i in range(2)]
                        ssb = [sb(es, f"at_s{i}", [128, 640], F32) for i in range(2)]
                        eb = [sb(es, f"at_e{i}", [128, 640], BF16) for i in range(2)]
                        rd = [sb(es, f"at_rd{i}", [128, 128], F32) for i in range(2)]
                        sps = [ps(es, f"at_sps{i}", [128, 1024]) for i in range(2)]
                        ods = [ps(es, f"at_od{i}", [128, 512]) for i in range(2)]
                        trp = [ps(es, f"at_tr{i}", [128, 1024], BF16) for i in range(2)]
                        if g == 1:
                            kcs = [[sb(es, f"at_kc{i}{s}", [128, 512], BF16) for s in range(2)] for i in range(2)]
                            vcs = [[sb(es, f"at_vc{i}{s}", [128, 4, 128], BF16) for s in range(2)] for i in range(2)]
                            bs = [sb(es, f"at_bs{i}", [128, 160], F32) for i in range(2)]
                            vn = [[sb(es, f"at_vn{i}{s}", [32, 128], BF16) for s in range(2)] for i in range(2)]
                        cnt = {"sp": 0, "od": 0, "tr": 0, "x": 0}

                        def load_head(h):
                            p = h % 2
                            r0 = (l * 16 + h) * 128
                            S.dma("sp", qh[p].h[:], q_d[h * 128:(h + 1) * 128, 0:Tg], reads=[DB("q", h)],
                                  writes=[qh[p].buf])
                            S.dma("sp", kh[p].h[:], k_d[r0:r0 + 128, C0:C0 + KW], reads=[DB("k", l, h)],
                                  writes=[kh[p].buf])
                            S.dma("sp", vh[p].h[:], v_d[r0:r0 + 128, C0:C0 + KW], reads=[DB("v", l, h)],
                                  writes=[vh[p].buf])
                            S.dma("sp", bp[p].h[:], biasp_d[l * 16 + h], writes=[bp[p].buf])
                            if g == 1:
                                S.dma("sp", bs[p].h[:], biass_d[l * 16 + h], writes=[bs[p].buf])
                                for s in range(2):
                                    S.dma("pool", kcs[p][s].h[:], ckT_d[(l * 2 + s) * 16 + h], writes=[kcs[p][s].buf])
                                    S.dma("pool", vcs[p][s].h[:],
                                          cv_d[l * 2 + s][:, h, :].rearrange("(j k) d -> k j d", k=128),
                                          writes=[vcs[p][s].buf])

                        load_head(0)
                        for h in range(16):
                            p = h % 2
                            if h + 1 < 16:
                                load_head(h + 1)
                            for t0 in range(0, NT, 4):
                                nt = min(4, NT - t0)
                                tp = trp[cnt["tr"] % 2]
                                cnt["tr"] += 1

                                def emit(e, t0=t0, nt=nt, tp=tp, p=p):
                                    for t in range(nt):
                                        ins = e.transpose(tp.h[:, t * 128:(t + 1) * 128],
                                                          vh[p].h[:, (t0 + t) * 128:(t0 + t + 1) * 128], ident_b.h[:])
                                    return ins

                                S.op("pe", emit, reads=[vh[p].buf, ident_b.buf], writes=[tp.buf])
                                S.op("act", lambda e, t0=t0, nt=nt, tp=tp, p=p: e.activation(
                                    out=vt[p].h[:, t0:t0 + nt, :],
                                    in_=tp.h[:, 0:nt * 128].rearrange("p (t d) -> p t d", d=128), func=AF.Copy),
                                     reads=[tp.buf], writes=[vt[p].buf])
                            if g == 1:
                                for s in range(2):
                                    tp = trp[cnt["tr"] % 2]
                                    cnt["tr"] += 1
                                    S.op("pe", lambda e, tp=tp, s=s, p=p: e.transpose(
                                        tp.h[0:32, 0:128], vh[p].h[:, 1536 + 32 * s:1568 + 32 * s], ident_b.h[:]),
                                         reads=[vh[p].buf, ident_b.buf], writes=[tp.buf])
                                    S.op("act", lambda e, tp=tp, s=s, p=p: e.activation(
                                        out=vn[p][s].h[:], in_=tp.h[0:32, 0:128], func=AF.Copy),
                                         reads=[tp.buf], writes=[vn[p][s].buf])
                            jobs = []
                            for ml in range(8):
                                m = 8 * g + ml
                                j0 = max(0, 4 - m)
                                x = cnt["x"] % 2
                                cnt["x"] += 1
                                sp_, od = sps[x], ods[x]
                                qa = qh[p].h[:, ml * 128:(ml + 1) * 128]

                                def stA_(sp_=sp_, x=x, j0=j0, m=m, qa=qa, p=p):
                                    def emit_s(e):
                                        for j in range(j0, 5):
                                            kc0 = 128 * (m - 4 + j) - C0
                                            ins = e.matmul(sp_.h[:, j * 128:(j + 1) * 128],
                                                           lhsT=kh[p].h[:, kc0:kc0 + 128], rhs=qa, start=True, stop=True)
                                        return ins

                                    S.op("pe", emit_s, reads=[kh[p].buf, qh[p].buf], writes=[sp_.buf])
                                    a0, a1 = j0 * 128, 640
                                    S.op("dve", lambda e: e.scalar_tensor_tensor(
                                        out=ssb[x].h[:, a0:a1], in0=sp_.h[:, a0:a1], scalar=SCALE,
                                        in1=bp[p].h[:, a0:a1], op0=ALU.mult, op1=ALU.add),
                                         reads=[sp_.buf, bp[p].buf], writes=[ssb[x].buf])
                                    S.op("act", lambda e: e.activation(out=eb[x].h[:, a0:a1], in_=ssb[x].h[:, a0:a1],
                                                                       func=AF.Exp),
                                         reads=[ssb[x].buf], writes=[eb[x].buf])

                                def stB_(od=od, x=x, j0=j0, m=m, p=p, h=h, ml=ml):
                                    def emit_o(e):
                                        for j in range(j0, 5):
                                            ti = m - 4 + j - C0 // 128
                                            e.matmul(od.h[:, 0:128], lhsT=vt[p].h[:, ti, :],
                                                     rhs=eb[x].h[:, j * 128:(j + 1) * 128], start=(j == j0), stop=(j == 4))
                                        for j in range(j0, 5):
                                            ins = e.matmul(od.h[:, 128:256], lhsT=ones_b.h[:],
                                                           rhs=eb[x].h[:, j * 128:(j + 1) * 128],
                                                           start=(j == j0), stop=(j == 4))
                                        return ins

                                    S.op("pe", emit_o, reads=[vt[p].buf, eb[x].buf, ones_b.buf], writes=[od.buf])
                                    S.op("dve", lambda e: e.reciprocal(out=rd[x].h[:], in_=od.h[:, 128:256]),
                                         reads=[od.buf], writes=[rd[x].buf])
                                    S.op("dve", lambda e: e.tensor_tensor(
                                        out=A.h[:, h, ml * 128:(ml + 1) * 128], in0=od.h[:, 0:128], in1=rd[x].h[:],
                                        op=ALU.mult), reads=[od.buf, rd[x].buf], writes=[A.sub[h]])

                                jobs.append((stA_, stB_))
                            if g == 1:
                                for s_i in range(2):
                                    x = cnt["x"] % 2
                                    cnt["x"] += 1
                                    sp_, od = sps[x], ods[x]
                                    qa = qh[p].h[:, 1024 + 32 * s_i:1056 + 32 * s_i]
                                    kn = kh[p].h[:, 1536 + 32 * s_i:1568 + 32 * s_i]

                                    def stA_(sp_=sp_, x=x, qa=qa, kn=kn, p=p, s=s_i):
                                        def emit_s(e):
                                            for jt in range(4):
                                                e.matmul(sp_.h[:, jt * 32:(jt + 1) * 32],
                                                         lhsT=kcs[p][s].h[:, jt * 128:(jt + 1) * 128], rhs=qa,
                                                         start=True, stop=True)
                                            return e.matmul(sp_.h[0:32, 128:160], lhsT=kn, rhs=qa, start=True, stop=True)

                                        S.op("pe", emit_s, reads=[kcs[p][s].buf, kh[p].buf, qh[p].buf], writes=[sp_.buf])
                                        for (r, a0, a1) in ((128, 0, 128), (32, 128, 160)):
                                            S.op("dve", lambda e, r=r, a0=a0, a1=a1: e.scalar_tensor_tensor(
                                                out=ssb[x].h[0:r, a0:a1], in0=sp_.h[0:r, a0:a1], scalar=SCALE,
                                                in1=bs[p].h[0:r, a0:a1], op0=ALU.mult, op1=ALU.add),
                                                 reads=[sp_.buf, bs[p].buf], writes=[ssb[x].buf])
                                            S.op("act", lambda e, r=r, a0=a0, a1=a1: e.activation(
                                                out=eb[x].h[0:r, a0:a1], in_=ssb[x].h[0:r, a0:a1], func=AF.Exp),
                                                 reads=[ssb[x].buf], writes=[eb[x].buf])

                                    def stB_(od=od, x=x, p=p, s=s_i, h=h):
                                        def emit_o(e):
                                            for jt in range(4):
                                                e.matmul(od.h[:, 0:32], lhsT=vcs[p][s].h[:, jt, :],
                                                         rhs=eb[x].h[:, jt * 32:(jt + 1) * 32], start=(jt == 0), stop=False)
                                            e.matmul(od.h[:, 0:32], lhsT=vn[p][s].h[:], rhs=eb[x].h[0:32, 128:160],
                                                     start=False, stop=True)
                                            for jt in range(4):
                                                e.matmul(od.h[:, 128:160], lhsT=ones_b.h[:],
                                                         rhs=eb[x].h[:, jt * 32:(jt + 1) * 32], start=(jt == 0), stop=False)
                                            return e.matmul(od.h[:, 128:160], lhsT=ones_b.h[0:32, :],
                                                            rhs=eb[x].h[0:32, 128:160], start=False, stop=True)

                                        S.op("pe", emit_o, reads=[vcs[p][s].buf, vn[p][s].buf, eb[x].buf, ones_b.buf],
                                             writes=[od.buf])
                                        S.op("dve", lambda e: e.reciprocal(out=rd[x].h[:, 0:32], in_=od.h[:, 128:160]),
                                             reads=[od.buf], writes=[rd[x].buf])
                                        S.op("dve", lambda e: e.tensor_tensor(
                                            out=A.h[:, h, 1024 + 32 * s:1056 + 32 * s], in0=od.h[:, 0:32],
                                            in1=rd[x].h[:, 0:32], op=ALU.mult),
                                             reads=[od.buf, rd[x].buf], writes=[A.sub[h]])

                                    jobs.append((stA_, stB_))
                            jobs[0][0]()
                            for t in range(len(jobs)):
                                if t + 1 < len(jobs):
                                    jobs[t + 1][0]()
                                jobs[t][1]()
                        S.run()

                    with ExitStack() as es:
                        S = Sched(gs)
                        ring = [sb(es, f"g2_w{i}", [128, 4096], BF16) for i in range(6)]
                        xr = [sb(es, f"g2_xr{i}", [128, Tg], F32) for i in range(2)]
                        orow = [sb(es, f"g2_or{i}", [128, Tg], F32) for i in range(2)]
                        sq = [sb(es, f"g2_sq{i}", [128, Tg], F32) for i in range(2)]
                        tmp = sb(es, "g2_tmp", [128, Tg], F32)
                        hl = [(sb(es, f"g2_hi{i}", [128, Tg], BF16), sb(es, f"g2_lo{i}", [128, Tg], BF16)) for i in range(2)]
                        pss = [ps(es, f"g2_ps{i}", [128, 512]) for i in range(4)]
                        st = [ps(es, f"g2_st{i}", [128, 512]) for i in range(len(tl))]
                        cat3 = cat_d.rearrange("(k p) t -> p k t", p=128)
                        for hf in range(4):
                            S.dma("sp", A.h[:, 16 + 4 * hf:20 + 4 * hf, :], cat3[:, 4 * hf:4 * hf + 4, 0:Tg],
                                  reads=[DB("cat", i) for i in range(4 * hf, 4 * hf + 4)],
                                  writes=[A.sub[k] for k in range(16 + 4 * hf, 20 + 4 * hf)])

                        def epi(i, ti, c, n, pt):
                            if ti == 0:
                                x_ = xr[i % 2]
                                S.dma("sp", x_.h[:], x_src[i * 128:(i + 1) * 128, G0:G0 + Tg], reads=[DB("x", i)],
                                      writes=[x_.buf])
                            S.op("dve", lambda e: e.tensor_tensor(out=orow[i % 2].h[:, c:c + n], in0=pt.h[:, 0:n],
                                                                  in1=xr[i % 2].h[:, c:c + n], op=ALU.add),
                                 reads=[pt.buf, xr[i % 2].buf], writes=[orow[i % 2].buf])

                        def epi_end(i):
                            o_ = orow[i % 2]
                            s_ = sq[i % 2]
                            S.dma("sp", x_d[i * 128:(i + 1) * 128, G0:G0 + Tg], o_.h[:], reads=[o_.buf],
                                  writes=[DB("x", i)])
                            S.op("act", lambda e: e.activation(out=s_.h[:], in_=o_.h[:], func=AF.Square),
                                 reads=[o_.buf], writes=[s_.buf])
                            stat_row(S, st, tl, s_, hl[i % 2], i == 0, i == 31, defer_to=i + 1)

                        run_gemm(S, A.h, A.sub, 32, tl, 32, lambda i, p: w_out[l * 32 + i], ring, pss, epi, epi_end, 2048)
                        rstd_from_stats(S, st, tl, 0, tmp)
                        S.run()

                with ExitStack() as es:
                    S = Sched(gs)
                    A = sb(es, "g3_A", [128, 32, Tg], BF16, nsub=32)
                    ring = [sb(es, f"g3_w{i}", [128, 4096], BF16) for i in range(6)]
                    xr = [sb(es, f"g3_xr{i}", [128, Tg], F32) for i in range(4)]
                    stA = [sb(es, f"g3_st{i}", [128, Tg], F32) for i in range(2)]
                    o16 = [sb(es, f"g3_o{i}", [128, Tg], BF16) for i in range(3)]
                    pss = [ps(es, f"g3_ps{i}", [128, 512]) for i in range(6)]
                    for blk in range(32):
                        x_ = xr[blk % 4]
                        S.dma("sp" if blk % 2 == 0 else "act", x_.h[:], x_d[blk * 128:(blk + 1) * 128, G0:G0 + Tg],
                              reads=[DB("x", blk)], writes=[x_.buf])
                        S.op("dve", lambda e, x_=x_, blk=blk: e.scalar_tensor_tensor(
                            out=A.h[:, blk, :], in0=x_.h[:], scalar=pcol("gffn", l * 32 + blk),
                            in1=rstd_b.h[:, 0:Tg], op0=ALU.mult, op1=ALU.mult),
                             reads=[x_.buf, rstd_b.buf], writes=[A.sub[blk]])

                    def epi(i, ti, c, n, pt):
                        f = i // 2
                        sa = stA[f % 2]
                        if i % 2 == 0:
                            S.op("act", lambda e: e.activation(out=sa.h[:, c:c + n], in_=pt.h[:, 0:n], func=AF.Silu),
                                 reads=[pt.buf], writes=[sa.buf])
                        else:
                            o_ = o16[f % 3]
                            S.op("dve", lambda e: e.tensor_tensor(out=o_.h[:, c:c + n], in0=pt.h[:, 0:n],
                                                                  in1=sa.h[:, c:c + n], op=ALU.mult),
                                 reads=[pt.buf, sa.buf], writes=[o_.buf])

                    def epi_end(i):
                        if i % 2 == 1:
                            f = i // 2
                            o_ = o16[f % 3]
                            S.dma("sp", hid_d[f * 128:(f + 1) * 128, 0:Tg], o_.h[:], reads=[o_.buf],
                                  writes=[DB("hid", f)])

                    run_gemm(S, A.h, A.sub, 32, tl, 172, lambda i, p: w_gu[l * 172 + i], ring, pss, epi, epi_end, 2048)
                    S.run()

                for sg in range(2):
                    c_lo = 512 * sg
                    Ts = 512 if sg == 0 else Tg - 512
                    tls = tiles_of(Ts)
                    with ExitStack() as es:
                        S = Sched(gs)
                        A = sb(es, "g4_A", [128, FC, Ts], BF16, nsub=22)
                        ring = [sb(es, f"g4_w{i}", [128, 5504], BF16) for i in range(5)]
                        xr = [sb(es, f"g4_xr{i}", [128, Ts], F32) for i in range(2)]
                        orow = [sb(es, f"g4_or{i}", [128, Ts], F32) for i in range(2)]
                        sq = [sb(es, f"g4_sq{i}", [128, Ts], F32) for i in range(2)]
                        tmp = sb(es, "g4_tmp", [128, Ts], F32)
                        hl = [(sb(es, f"g4_hi{i}", [128, Ts], BF16), sb(es, f"g4_lo{i}", [128, Ts], BF16)) for i in range(2)]
                        pss = [ps(es, f"g4_ps{i}", [128, 512]) for i in range(4)]
                        st = [ps(es, f"g4_st{i}", [128, 512]) for i in range(len(tls))]
                        hid3 = hid_d.rearrange("(k p) t -> p k t", p=128)
                        for hf in range(22):
                            k0, k1 = 4 * hf, min(FC, 4 * hf + 4)
                            S.dma("sp", A.h[:, k0:k1, :], hid3[:, k0:k1, c_lo:c_lo + Ts],
                                  reads=[DB("hid", f) for f in range(k0, k1)], writes=[A.sub[hf]])

                        def epi(i, ti, c, n, pt):
                            if ti == 0:
                                x_ = xr[i % 2]
                                S.dma("sp", x_.h[:], x_d[i * 128:(i + 1) * 128, G0 + c_lo:G0 + c_lo + Ts],
                                      reads=[DB("x", i)], writes=[x_.buf])
                            S.op("dve", lambda e: e.tensor_tensor(out=orow[i % 2].h[:, c:c + n], in0=pt.h[:, 0:n],
                                                                  in1=xr[i % 2].h[:, c:c + n], op=ALU.add),
                                 reads=[pt.buf, xr[i % 2].buf], writes=[orow[i % 2].buf])

                        def epi_end(i):
                            o_ = orow[i % 2]
                            s_ = sq[i % 2]
                            S.dma("sp", x_d[i * 128:(i + 1) * 128, G0 + c_lo:G0 + c_lo + Ts], o_.h[:], reads=[o_.buf],
                                  writes=[DB("x", i)])
                            S.op("act", lambda e: e.activation(out=s_.h[:], in_=o_.h[:], func=AF.Square),
                                 reads=[o_.buf], writes=[s_.buf])
                            stat_row(S, st, tls, s_, hl[i % 2], i == 0, i == 31, defer_to=i + 1)

                        run_gemm(S, A.h, A.sub, FC, tls, 32, lambda i, p: w_dn[l * 32 + i][:, p * 5504:(p + 1) * 5504], ring, pss, epi, epi_end, 1376, P=2)
                        rstd_from_stats(S, st, tls, c_lo, tmp)
                        S.run()

            with ExitStack() as es:
                S = Sched(gs)
                xr = [sb(es, f"fn_x{i}", [128, Tg], F32) for i in range(3)]
                orow = [sb(es, f"fn_o{i}", [128, Tg], F32) for i in range(3)]
                for blk in range(32):
                    x_ = xr[blk % 3]
                    o_ = orow[blk % 3]
                    S.dma("sp", x_.h[:], x_d[blk * 128:(blk + 1) * 128, G0:G0 + Tg], reads=[DB("x", blk)],
                          writes=[x_.buf])
                    S.op("dve", lambda e, x_=x_, o_=o_, blk=blk: e.scalar_tensor_tensor(
                        out=o_.h[:], in0=x_.h[:], scalar=pcol("gfin", blk), in1=rstd_b.h[:, 0:Tg],
                        op0=ALU.mult, op1=ALU.mult), reads=[x_.buf, rstd_b.buf], writes=[o_.buf])
                    S.dma("sp", yT[blk * 128:(blk + 1) * 128, G0:G0 + Tg], o_.h[:], reads=[o_.buf])
                if g == 1:
                    S.dma("sp", cbo_d, cbo_sb.h[:], reads=[cbo_sb.buf])
                    S.dma("sp", cco_d, cco_sb.h[:], reads=[cco_sb.buf])
                S.run()
    return nc


def _blockify(w, perm=None):
    K, N = w.shape
    a = w.reshape(K // 128, 128, N // 128, 128).transpose(2, 1, 0, 3)
    if perm is not None:
        a = a[perm]
    return np.ascontiguousarray(a).reshape(a.shape[0], 128, K)


def _pvec(v):
    return np.ascontiguousarray(v.reshape(-1, 128).T)


_CACHE = {}


def _prep(x_prompt, x_sample, cache_attn_k, cache_attn_v, cache_conv_b, cache_conv_c,
          norm_mix_g, w_in, rel_bias, conv_b_w, conv_c_w, conv_c_b, ln_c_g, ln_c_b,
          w_out, norm_ffn_g, w_ffn_gate, w_ffn_up, w_ffn_down, final_norm_g, cores=range(8)):
    f = lambda a: np.asarray(a, dtype=np.float32)
    x_prompt, x_sample = f(x_prompt), f(x_sample)
    cache_attn_k, cache_attn_v = f(cache_attn_k), f(cache_attn_v)
    cache_conv_b, cache_conv_c = f(cache_conv_b), f(cache_conv_c)
    rel_bias = f(rel_bias)
    n_cores = 8

    perm = [ORIG_BLK[k] + i for (k, i) in IN_ORDER]
    w_in_b = np.concatenate([_blockify(f(w_in[l]), perm) for l in range(L)], axis=0)
    w_out_b = np.concatenate([_blockify(f(w_out[l])) for l in range(L)], axis=0)
    gu = []
    for l in range(L):
        gb = _blockify(f(w_ffn_gate[l]))
        ub = _blockify(f(w_ffn_up[l]))
        gu.append(np.stack([gb, ub], axis=1).reshape(172, 128, 4096))
    w_gu_b = np.concatenate(gu, axis=0)
    w_dn_b = np.concatenate([_blockify(f(w_ffn_down[l])) for l in range(L)], axis=0)

    k = np.arange(128)[:, None, None]
    j = np.arange(5)[None, :, None]
    q = np.arange(128)[None, None, :]
    d = q - k + 128 * (4 - j)
    idx = np.clip(d, -256, 256) + 256
    masked = ((j == 4) & (k >= 64) & (q < 64)) | ((j == 0) & (k < 64) & (q >= 64))
    biasp = np.where(masked[None, None], np.float32(NEG), rel_bias[:, :, idx]).astype(np.float32)
    biasp = np.ascontiguousarray(biasp.reshape(L * 16, 128, 640))
    q32 = np.arange(32)[None, None, :]
    ds = np.where(j < 4, q32 + 512 - (128 * j + k), q32 - k)
    idxs = np.clip(ds, -256, 256) + 256
    biass = np.ascontiguousarray(rel_bias[:, :, idxs].astype(np.float32).reshape(L * 16, 128, 160))
    ident = np.eye(128, dtype=np.float32)

    def par_common():
        p = np.zeros((128, NPAR), np.float32)
        for l in range(L):
            p[:, OFF["gmix"] + l * 32: OFF["gmix"] + (l + 1) * 32] = _pvec(f(norm_mix_g[l]))
            p[:, OFF["gffn"] + l * 32: OFF["gffn"] + (l + 1) * 32] = _pvec(f(norm_ffn_g[l]))
            cb = f(conv_b_w[l]).reshape(3, 8, 128).transpose(2, 1, 0).reshape(128, 24)
            p[:, OFF["cbw"] + l * 24: OFF["cbw"] + (l + 1) * 24] = cb
            cc = f(conv_c_w[l]).reshape(31, 8, 128).transpose(2, 1, 0).reshape(128, 248)
            p[:, OFF["ccw"] + l * 248: OFF["ccw"] + (l + 1) * 248] = cc
            p[:, OFF["ccb"] + l * 8: OFF["ccb"] + (l + 1) * 8] = _pvec(f(conv_c_b[l]))
            p[:, OFF["lng"] + l * 8: OFF["lng"] + (l + 1) * 8] = _pvec(f(ln_c_g[l]))
            p[:, OFF["lnb"] + l * 8: OFF["lnb"] + (l + 1) * 8] = _pvec(f(ln_c_b[l]))
        p[:, OFF["gfin"]: OFF["gfin"] + 32] = _pvec(f(final_norm_g))
        return p

    pc = par_common()
    in_maps = []
    for c in cores:
        xTc = np.ascontiguousarray(np.concatenate(
            [x_prompt[c].T, x_sample[2 * c].T, x_sample[2 * c + 1].T], axis=1))
        p = pc.copy()
        hb = cache_conv_b[:, 2 * c:2 * c + 2].reshape(L, 2, 2, 8, 128).transpose(4, 0, 1, 3, 2).reshape(128, -1)
        hc = cache_conv_c[:, 2 * c:2 * c + 2].reshape(L, 2, 30, 8, 128).transpose(4, 0, 1, 3, 2).reshape(128, -1)
        p[:, OFF["histb"]:OFF["histb"] + hb.shape[1]] = hb
        p[:, OFF["histc"]:OFF["histc"] + hc.shape[1]] = hc
        ck = cache_attn_k[:, 2 * c:2 * c + 2]
        ckT = np.ascontiguousarray(ck.transpose(0, 1, 3, 4, 2)).reshape(L * 2 * 16, 128, 512)
        cv = np.ascontiguousarray(cache_attn_v[:, 2 * c:2 * c + 2]).reshape(L * 2, 512, 16, 128)
        in_maps.append({"xT": xTc, "w_in": w_in_b, "w_out": w_out_b, "w_gu": w_gu_b, "w_dn": w_dn_b, "par": p,
                        "biasp": biasp, "biass": biass, "ckT": ckT, "cv": cv, "ident": ident})

    return in_maps


def kernel(**inputs):
    n_cores = 8
    in_maps = _prep(**inputs)
    if "nc" not in _CACHE:
        _CACHE["nc"] = build_program()
    nc = _CACHE["nc"]
    res = run_bass_kernel_spmd(nc, in_maps, core_ids=list(range(n_cores)))
    R = res.results

    y_prompt = np.empty((8, 2048, D), np.float32)
    y_sample = np.empty((16, 32, D), np.float32)
    pk = np.empty((L, 8, 512, 16, 128), np.float32)
    pv = np.empty_like(pk)
    sk = np.empty((L, 16, 32, 16, 128), np.float32)
    sv = np.empty_like(sk)
    pb = np.empty((L, 8, 2, 1024), np.float32)
    pcv = np.empty((L, 8, 30, 1024), np.float32)
    sbo = np.empty((L, 16, 2, 1024), np.float32)
    sco = np.empty((L, 16, 30, 1024), np.float32)
    for c in range(n_cores):
        r = R[c]
        yT = r["yT"]
        y_prompt[c] = yT[:, :2048].T
        for s in range(2):
            y_sample[2 * c + s] = yT[:, 2048 + 32 * s:2080 + 32 * s].T
        for name, P_, S_ in (("kTo", pk, sk), ("vTo", pv, sv)):
            a = r[name].reshape(L, 2048, 576)
            for l in range(L):
                P_[l, c] = a[l][:, :512].T.reshape(512, 16, 128)
                for s in range(2):
                    S_[l, 2 * c + s] = a[l][:, 512 + 32 * s:544 + 32 * s].T.reshape(32, 16, 128)
        cb = r["cbo"].reshape(128, L, 8, 3, 2)
        cc = r["cco"].reshape(128, L, 8, 3, 30)
        for l in range(L):
            pb[l, c] = cb[:, l, :, 0, :].transpose(2, 1, 0).reshape(2, 1024)
            pcv[l, c] = cc[:, l, :, 0, :].transpose(2, 1, 0).reshape(30, 1024)
            for s in range(2):
                sbo[l, 2 * c + s] = cb[:, l, :, 1 + s, :].transpose(2, 1, 0).reshape(2, 1024)
                sco[l, 2 * c + s] = cc[:, l, :, 1 + s, :].transpose(2, 1, 0).reshape(30, 1024)
    return (y_prompt, y_sample, pk, pv, pb, pcv, sk, sv, sbo, sco)
```

```python
import numpy as np
from contextlib import ExitStack
import concourse.bass as bass
import concourse.mybir as mybir
from concourse.bass_utils import run_bass_kernel_spmd

F32 = mybir.dt.float32
BF16 = mybir.dt.bfloat16
AF = mybir.ActivationFunctionType
ALU = mybir.AluOpType

L = 2
D = 4096
KC = 32
DFF = 11008
FC = 86
H = 16
TOK = 2112
TG = (1024, 1088)
EPS = 1e-6
SCALE = 128 ** -0.5
NEG = -30000.0
_STOP = None
_NB1 = None
_NOSTAT = False

OFF = {}
_o = 0
for _n, _sz in [("gmix", L * 32), ("gffn", L * 32), ("gfin", 32), ("cbw", L * 8 * 3), ("ccw", L * 8 * 31),
                ("ccb", L * 8), ("lng", L * 8), ("lnb", L * 8), ("histb", L * 2 * 8 * 2),
                ("histc", L * 2 * 8 * 30)]:
    OFF[_n] = _o
    _o += _sz
NPAR = _o

_QKV = [("q", h) for h in range(16)] + [("k", h) for h in range(16)] + [("v", h) for h in range(16)]
IN_ORDER = []
for _i in range(8):
    IN_ORDER += [("cg", _i), ("ca", _i)] + _QKV[6 * _i:6 * _i + 6] + [("sc", _i), ("sh", _i), ("sb", _i)]
ORIG_BLK = {"q": 0, "k": 16, "v": 32, "sb": 48, "sc": 56, "sh": 64, "ca": 72, "cg": 80}


class Tok:
    __slots__ = ("eng", "sem", "value", "marked", "is_dma")

    def __init__(self, eng, sem=None, value=None, is_dma=False):
        self.eng = eng
        self.sem = sem
        self.value = value
        self.marked = is_dma
        self.is_dma = is_dma


class Buf:
    __slots__ = ("w", "r")

    def __init__(self):
        self.w = None
        self.r = []


class GS:
    NPOOL = 20

    def __init__(self, nc, es):
        self.nc = nc
        self.eng_sem = {e: es.enter_context(nc.semaphore("sem_" + e)) for e in ("pe", "act", "dve", "pool")}
        self.cnt = {e: 0 for e in self.eng_sem}
        self.dma_sems = {q: [es.enter_context(nc.semaphore(f"dq_{q}{i}")) for i in range(self.NPOOL)]
                         for q in ("sp", "pool", "act")}
        self.dma_cnt = {q: [0] * self.NPOOL for q in self.dma_sems}
        self.dma_rr = {q: 0 for q in self.dma_sems}
        self.known = {e: {} for e in ("pe", "act", "dve", "pool", "sp")}
        self.bufs = []
        self.phase = 0

    def buf(self):
        b = Buf()
        self.bufs.append(b)
        return b


class Sched:
    ENGS = ("pe", "act", "dve", "pool", "sp")
    BLK = {"pe": "tensor", "act": "scalar", "dve": "vector", "pool": "gpsimd", "sp": "sync"}

    def __init__(self, gs):
        self.gs = gs
        self.nc = gs.nc
        self.ops = {e: [] for e in self.ENGS}
        self.used_dma = {e: {} for e in self.ENGS}

    def _deps(self, eng, reads, writes):
        deps = []
        for b in reads:
            if b.w is not None:
                deps.append(b.w)
        for b in writes:
            if b.w is not None:
                deps.append(b.w)
            deps.extend(b.r)
        out = []
        for d in deps:
            if (not d.is_dma) and d.eng == "pe" and eng == "pe":
                continue
            d.marked = True
            out.append(d)
        return out

    @staticmethod
    def _post(tok, reads, writes):
        for b in reads:
            b.r.append(tok)
        for b in writes:
            b.w = tok
            b.r = []

    def op(self, eng, emit, reads=(), writes=()):
        deps = self._deps(eng, reads, writes)
        tok = Tok(eng, self.gs.eng_sem[eng])
        self.ops[eng].append((deps, emit, tok))
        self._post(tok, reads, writes)
        return tok

    def dma(self, q, out, in_, reads=(), writes=()):
        gs = self.gs
        deps = self._deps(q, reads, writes)
        i = gs.dma_rr[q]
        gs.dma_rr[q] = (i + 1) % gs.NPOOL
        sem = gs.dma_sems[q][i]
        prev = gs.dma_cnt[q][i]
        gs.dma_cnt[q][i] = prev + 16
        tok = Tok(q, sem, prev + 16, is_dma=True)
        if prev > 0:
            deps.append(Tok(q, sem, prev, is_dma=True))
        self.used_dma[q][i] = (sem, prev + 16)
        self.ops[q].append((deps, (lambda e, o=out, s=in_: e.dma_start(out=o, in_=s)), tok))
        self._post(tok, reads, writes)
        return tok

    def run(self):
        gs = self.gs
        gs.phase += 1
        if _STOP is not None and gs.phase > _STOP:
            for b in gs.bufs:
                b.w = None
                b.r = []
            return
        for e in ("pe", "act", "dve", "pool"):
            for deps, emit, tok in self.ops[e]:
                if (not tok.is_dma) and tok.marked:
                    gs.cnt[e] += 1
                    tok.value = gs.cnt[e]
        with self.nc.Block() as blk:
            for e in self.ENGS:
                if not self.ops[e]:
                    continue

                def body(eh, e=e):
                    known = gs.known[e]
                    for deps, emit, tok in self.ops[e]:
                        need = {}
                        for d in deps:
                            k = id(d.sem)
                            if k not in need or need[k][1] < d.value:
                                need[k] = (d.sem, d.value)
                        for k, (sh, v) in need.items():
                            if known.get(k, 0) < v:
                                eh.wait_ge(sh, v)
                                known[k] = v
                        ins = emit(eh)
                        if tok.is_dma:
                            ins.then_inc(tok.sem, 16)
                        elif tok.marked:
                            ins.then_inc(tok.sem, 1)
                    for i, (sh, v) in self.used_dma[e].items():
                        k = id(sh)
                        if known.get(k, 0) < v:
                            eh.wait_ge(sh, v)
                            known[k] = v

                getattr(blk, self.BLK[e])(body)
        for b in gs.bufs:
            b.w = None
            b.r = []


class T:
    def __init__(self, gs, h, nsub=0):
        self.h = h
        self.buf = gs.buf()
        self.sub = [gs.buf() for _ in range(nsub)]


def tiles_eq(n):
    k = -(-n // 512)
    base = (n // k) & ~1
    ws = [base] * k
    ws[0] += n - base * k
    out = []
    c = 0
    for w in ws:
        out.append((c, w))
        c += w
    return out


def tiles_of(n):
    out = []
    c = 0
    while c < n:
        w = min(512, n - c)
        out.append((c, w))
        c += w
    return out


def build_program():
    nc = bass.Bass("TRN2", target_bir_lowering=False)
    di = lambda n, s, dt=F32: nc.dram_tensor(n, s, dt, kind="ExternalInput").ap()
    do = lambda n, s, dt=F32: nc.dram_tensor(n, s, dt, kind="ExternalOutput").ap()
    dx = lambda n, s, dt: nc.dram_tensor(n, s, dt).ap()

    xT = di("xT", [D, TOK])
    w_in = di("w_in", [L * 88, 128, 4096])
    w_out = di("w_out", [L * 32, 128, 4096])
    w_gu = di("w_gu", [L * 172, 128, 4096])
    w_dn = di("w_dn", [L * 32, 128, DFF])
    par_d = di("par", [128, NPAR])
    biasp_d = di("biasp", [L * 16, 128, 640])
    biass_d = di("biass", [L * 16, 128, 160])
    ckT_d = di("ckT", [L * 2 * 16, 128, 512])
    cv_d = di("cv", [L * 2, 512, 16, 128])
    ident_d = di("ident", [128, 128])

    yT = do("yT", [D, TOK])
    kTo = do("kTo", [L * 2048, 576])
    vTo = do("vTo", [L * 2048, 576])
    cbo_d = do("cbo", [128, L * 8 * 6])
    cco_d = do("cco", [128, L * 8 * 90])

    x_d = dx("x_d", [D, TOK], F32)
    q_d = dx("q_d", [16 * 128, 1088], BF16)
    k_d = dx("k_d", [L * 16 * 128, TOK], BF16)
    v_d = dx("v_d", [L * 16 * 128, TOK], BF16)
    cat_d = dx("cat_d", [2048, 1088], BF16)
    zc_d = dx("zc_d", [1024, 1088], F32)
    hid_d = dx("hid_d", [2 * 128, FC * 576], BF16)

    with ExitStack() as top:
        gs = GS(nc, top)
        uid = [0]

        def uname(n):
            uid[0] += 1
            return f"{n}_u{uid[0]}"

        sb = lambda es, n, s, dt, nsub=0: T(gs, es.enter_context(nc.sbuf_tensor(uname(n), s, dt)), nsub)
        ps = lambda es, n, s, dt=F32: T(gs, es.enter_context(nc.psum_tensor(uname(n), s, dt)))

        par = sb(top, "par", [128, NPAR], F32)
        ones_b = sb(top, "ones_b", [128, 128], BF16)
        ident_b = sb(top, "ident_b", [128, 128], BF16)
        rstd_b = sb(top, "rstd_b", [128, 1088], F32)
        mean_b = sb(top, "mean_b", [128, 1088], F32)
        halo_b = sb(top, "halo_b", [128, L * 8 * 2], F32)
        halo_c = sb(top, "halo_c", [128, L * 8 * 30], F32)
        cbo_sb = sb(top, "cbo_sb", [128, L * 8 * 6], F32)
        cco_sb = sb(top, "cco_sb", [128, L * 8 * 90], F32)

        dbuf = {}

        def DB(*key):
            if key not in dbuf:
                dbuf[key] = gs.buf()
            return dbuf[key]

        pcol = lambda name, idx: par.h[:, OFF[name] + idx: OFF[name] + idx + 1]

        S = Sched(gs)
        S.dma("sp", par.h[:], par_d, writes=[par.buf])
        S.dma("pool", ident_b.h[:], ident_d, writes=[ident_b.buf])
        S.op("dve", lambda e: e.memset(ones_b.h[:], 1.0), writes=[ones_b.buf])
        S.run()

        def rstd_from_stats(S, stat_tiles, tl, col0, tmp):
            for (c, n), st in zip(tl, stat_tiles):
                S.op("dve", lambda e, c=c, n=n, st=st: e.tensor_scalar(
                    out=tmp.h[:, c:c + n], in0=st.h[:, 0:n], scalar1=1.0 / D, scalar2=EPS,
                    op0=ALU.mult, op1=ALU.add), reads=[st.buf], writes=[tmp.buf])
                S.op("act", lambda e, c=c, n=n: e.activation(out=tmp.h[:, c:c + n], in_=tmp.h[:, c:c + n],
                                                             func=AF.Sqrt), reads=[tmp.buf], writes=[tmp.buf])
                S.op("dve", lambda e, c=c, n=n: e.reciprocal(out=rstd_b.h[:, col0 + c: col0 + c + n],
                                                             in_=tmp.h[:, c:c + n]),
                     reads=[tmp.buf], writes=[rstd_b.buf])

        def stat_row(S, sts, tl_, src, hl, first, last, defer_to=None):
            if _NOSTAT:
                return
            hi, lo = hl
            S.op("act", lambda e: e.activation(out=hi.h[:], in_=src.h[:], func=AF.Copy),
                 reads=[src.buf], writes=[hi.buf])
            S.op("dve", lambda e: e.tensor_tensor(out=lo.h[:], in0=src.h[:], in1=hi.h[:], op=ALU.subtract),
                 reads=[src.buf, hi.buf], writes=[lo.buf])
            def pe_part():
                for (c, n), st_ in zip(tl_, sts):
                    def emit(e, c=c, n=n, st_=st_):
                        e.matmul(st_.h[:, 0:n], lhsT=ones_b.h[:], rhs=hi.h[:, c:c + n], start=first, stop=False)
                        return e.matmul(st_.h[:, 0:n], lhsT=ones_b.h[:], rhs=lo.h[:, c:c + n], start=False, stop=last)
                    S.op("pe", emit, reads=[hi.buf, lo.buf, ones_b.buf], writes=[st_.buf])

            if defer_to is None:
                pe_part()
            else:
                S.deferred.append((defer_to, pe_part))

        def run_gemm(S, A3, a_bufs, kcn, tl, nblocks, wsrc, ring, pss, epi, epi_end, esz, P=1):
            R = len(ring)
            kp = kcn // P
            total = nblocks * P
            S.deferred = []
            nl = [0]

            def load(j):
                slot = ring[j % R]
                S.dma("pool", slot.h[:, 0:kp * 128].rearrange("p (c e) -> p c e", e=esz),
                      wsrc(j // P, j % P).rearrange("p (c e) -> p c e", e=esz), writes=[slot.buf])

            def flush(i):
                cur = S.deferred
                S.deferred = []
                for (tgt, th) in cur:
                    if tgt <= i:
                        th()
                    else:
                        S.deferred.append((tgt, th))

            pi = 0
            for i in range(nblocks):
                while nl[0] < total and nl[0] < i * P + R:
                    load(nl[0])
                    nl[0] += 1
                slots = [ring[(i * P + p) % R] for p in range(P)]
                w3s = [sl.h[:, 0:kp * 128].rearrange("p (k n) -> p k n", n=128) for sl in slots]
                for ti, (c, n) in enumerate(tl):
                    pt = pss[pi % len(pss)]
                    pi += 1

                    def emit(e, w3s=w3s, pt=pt, c=c, n=n):
                        for kc in range(kcn):
                            ins = e.matmul(pt.h[:, 0:n], lhsT=w3s[kc // kp][:, kc % kp, :], rhs=A3[:, kc, c:c + n],
                                           start=(kc == 0), stop=(kc == kcn - 1))
                        return ins

                    S.op("pe", emit, reads=list(a_bufs) + [sl.buf for sl in slots], writes=[pt.buf])
                    epi(i, ti, c, n, pt)
                flush(i)
                epi_end(i)
            while S.deferred:
                flush(10 ** 9)

        for g in range(2):
            Tg = TG[g]
            tl = tiles_of(Tg)
            G0 = 1024 * g

            with ExitStack() as es:
                S = Sched(gs)
                xr = [sb(es, f"s0_xr{i}", [128, Tg], F32) for i in range(3)]
                sq = [sb(es, f"s0_sq{i}", [128, Tg], F32) for i in range(2)]
                tmp = sb(es, "s0_tmp", [128, Tg], F32)
                hl = [(sb(es, f"s0_hi{i}", [128, Tg], BF16), sb(es, f"s0_lo{i}", [128, Tg], BF16)) for i in range(2)]
                st = [ps(es, f"s0_st{i}", [128, 512]) for i in range(len(tl))]
                for blk in range(32):
                    x_ = xr[blk % 3]
                    s_ = sq[blk % 2]
                    S.dma("sp", x_.h[:], xT[blk * 128:(blk + 1) * 128, G0:G0 + Tg], writes=[x_.buf])
                    S.op("act", lambda e, x_=x_, s_=s_: e.activation(out=s_.h[:], in_=x_.h[:], func=AF.Square),
                         reads=[x_.buf], writes=[s_.buf])
                    stat_row(S, st, tl, s_, hl[blk % 2], blk == 0, blk == 31)
                rstd_from_stats(S, st, tl, 0, tmp)
                S.run()

            for l in range(L):
                x_src = xT if l == 0 else x_d

                with ExitStack() as es:
                    S = Sched(gs)
                    A = sb(es, "g1_A", [128, 32, Tg], BF16, nsub=32)
                    ring = [sb(es, f"g1_w{i}", [128, 4096], BF16) for i in range(5)]
                    hl = [(sb(es, f"g1_hi{i}", [128, Tg], BF16), sb(es, f"g1_lo{i}", [128, Tg], BF16)) for i in range(2)]
                    xr = [sb(es, f"g1_xr{i}", [128, Tg], F32) for i in range(4)]
                    stA = sb(es, "g1_stA", [128, Tg], F32)
                    UL = 1094 if g == 1 else 1026
                    CL = 1178 if g == 1 else 1054
                    ubuf = sb(es, "g1_ubuf", [128, UL], F32)
                    cbuf = sb(es, "g1_cbuf", [128, CL], F32)
                    acc = [sb(es, f"g1_acc{i}", [128, CL], F32) for i in range(2)]
                    zcrow = [sb(es, f"g1_zc{i}", [128, Tg], F32) for i in range(2)]
                    o16 = [sb(es, f"g1_o16{i}", [128, Tg], BF16) for i in range(3)]
                    okv = [sb(es, f"g1_okv{i}", [128, 576], F32) for i in range(2)]
                    pss = [ps(es, f"g1_ps{i}", [128, 512]) for i in range(4)]
                    tl1 = tiles_eq(Tg)
                    st = [ps(es, f"g1_st{i}", [128, 512]) for i in range(len(tl1))]

                    for blk in range(32):
                        x_ = xr[blk % 4]
                        S.dma("sp" if blk % 2 == 0 else "act", x_.h[:], x_src[blk * 128:(blk + 1) * 128, G0:G0 + Tg],
                              reads=[DB("x", blk)], writes=[x_.buf])
                        S.op("dve", lambda e, x_=x_, blk=blk: e.scalar_tensor_tensor(
                            out=A.h[:, blk, :], in0=x_.h[:], scalar=pcol("gmix", l * 32 + blk),
                            in1=rstd_b.h[:, 0:Tg], op0=ALU.mult, op1=ALU.mult),
                             reads=[x_.buf, rstd_b.buf], writes=[A.sub[blk]])

                    if g == 0:
                        S.op("dve", lambda e: e.memset(ubuf.h[:, 0:2], 0.0), writes=[ubuf.buf])
                        S.op("dve", lambda e: e.memset(cbuf.h[:, 0:30], 0.0), writes=[cbuf.buf])
                    state = {"o16": 0, "okv": 0, "acc": 0, "zc": 0, "nC": 0}

                    def seg_map(c, n, upos, spos):
                        out = []
                        for (lo, hi, base) in ((0, 1024, upos), (1024, 1056, spos[0]), (1056, 1088, spos[1])):
                            o_lo, o_hi = max(c, lo), min(c + n, hi)
                            if o_lo < o_hi:
                                out.append((o_lo - c, o_hi - c, base + (o_lo - lo)))
                        return out

                    def epi(i, ti, c, n, pt):
                        kind, idx = IN_ORDER[i]
                        if kind in ("q", "k", "v"):
                            o_ = o16[state["o16"] % 3]
                            S.op("act", lambda e: e.activation(out=o_.h[:, c:c + n], in_=pt.h[:, 0:n], func=AF.Copy),
                                 reads=[pt.buf], writes=[o_.buf])
                            if kind != "q" and g == 1 and c + n > 512:
                                ok = okv[state["okv"] % 2]
                                o_lo = max(c, 512)
                                S.op("act", lambda e: e.activation(out=ok.h[:, o_lo - 512:c + n - 512],
                                                                   in_=pt.h[:, o_lo - c:n], func=AF.Copy),
                                     reads=[pt.buf], writes=[ok.buf])
                        elif kind == "sc":
                            S.op("act", lambda e: e.activation(out=stA.h[:, c:c + n], in_=pt.h[:, 0:n], func=AF.Copy),
                                 reads=[pt.buf], writes=[stA.buf])
                        elif kind == "sh":
                            for (a, b, dst) in seg_map(c, n, 2, (1028, 1062)):
                                S.op("dve", lambda e, a=a, b=b, dst=dst: e.tensor_tensor(
                                    out=ubuf.h[:, dst:dst + (b - a)], in0=pt.h[:, a:b], in1=stA.h[:, c + a:c + b],
                                    op=ALU.mult), reads=[pt.buf, stA.buf], writes=[ubuf.buf])
                        elif kind == "sb":
                            o_ = o16[state["o16"] % 3]
                            ac = acc[state["acc"] % 2]
                            for (a, b, src) in seg_map(c, n, 0, (1026, 1060)):
                                S.op("dve", lambda e, a=a, b=b, src=src: e.tensor_tensor(
                                    out=o_.h[:, c + a:c + b], in0=pt.h[:, a:b], in1=ac.h[:, src:src + (b - a)],
                                    op=ALU.mult), reads=[pt.buf, ac.buf], writes=[o_.buf])
                        elif kind == "cg":
                            S.op("act", lambda e: e.activation(out=stA.h[:, c:c + n], in_=pt.h[:, 0:n],
                                                               func=AF.Sigmoid),
                                 reads=[pt.buf], writes=[stA.buf])
                        elif kind == "ca":
                            for (a, b, dst) in seg_map(c, n, 30, (1084, 1146)):
                                S.op("dve", lambda e, a=a, b=b, dst=dst: e.tensor_tensor(
                                    out=cbuf.h[:, dst:dst + (b - a)], in0=pt.h[:, a:b], in1=stA.h[:, c + a:c + b],
                                    op=ALU.mult), reads=[pt.buf, stA.buf], writes=[cbuf.buf])

                    def epi_end(i):
                        kind, idx = IN_ORDER[i]
                        if kind in ("q", "k", "v"):
                            o_ = o16[state["o16"] % 3]
                            state["o16"] += 1
                            if kind == "q":
                                S.dma("sp", q_d[idx * 128:(idx + 1) * 128, 0:Tg], o_.h[:], reads=[o_.buf],
                                      writes=[DB("q", idx)])
                            else:
                                dst = k_d if kind == "k" else v_d
                                r0 = (l * 16 + idx) * 128
                                S.dma("sp", dst[r0:r0 + 128, G0:G0 + Tg], o_.h[:], reads=[o_.buf],
                                      writes=[DB(kind, l, idx)])
                                if g == 1:
                                    ok = okv[state["okv"] % 2]
                                    state["okv"] += 1
                                    od = kTo if kind == "k" else vTo
                                    r1 = l * 2048 + idx * 128
                                    S.dma("sp", od[r1:r1 + 128, :], ok.h[:], reads=[ok.buf])
                        elif kind == "sh":
                            if g == 1:
                                hb = OFF["histb"] + (l * 2 * 8 + idx) * 2
                                S.op("dve", lambda e: e.tensor_copy(out=ubuf.h[:, 0:2],
                                                                    in_=halo_b.h[:, (l * 8 + idx) * 2:(l * 8 + idx) * 2 + 2]),
                                     reads=[halo_b.buf], writes=[ubuf.buf])
                                for s in range(2):
                                    hb = OFF["histb"] + ((l * 2 + s) * 8 + idx) * 2
                                    S.op("dve", lambda e, s=s, hb=hb: e.tensor_copy(
                                        out=ubuf.h[:, 1026 + 34 * s:1028 + 34 * s], in_=par.h[:, hb:hb + 2]),
                                         writes=[ubuf.buf])
                            ac = acc[state["acc"] % 2]
                            n_out = UL - 2
                            w = lambda j: pcol("cbw", (l * 8 + idx) * 3 + j)
                            S.op("dve", lambda e: e.tensor_scalar(out=ac.h[:, 0:n_out], in0=ubuf.h[:, 0:n_out],
                                                                  scalar1=w(0), scalar2=None, op0=ALU.mult),
                                 reads=[ubuf.buf], writes=[ac.buf])
                            for j in (1, 2):
                                S.op("dve", lambda e, j=j: e.scalar_tensor_tensor(
                                    out=ac.h[:, 0:n_out], in0=ubuf.h[:, j:j + n_out], scalar=w(j),
                                    in1=ac.h[:, 0:n_out], op0=ALU.mult, op1=ALU.add),
                                     reads=[ubuf.buf, ac.buf], writes=[ac.buf])
                            if g == 0:
                                S.op("dve", lambda e: e.tensor_copy(
                                    out=halo_b.h[:, (l * 8 + idx) * 2:(l * 8 + idx) * 2 + 2], in_=ubuf.h[:, 1024:1026]),
                                     reads=[ubuf.buf], writes=[halo_b.buf])
                            else:
                                for s3, src in enumerate((1024, 1058, 1092)):
                                    o0 = (l * 8 + idx) * 6 + s3 * 2
                                    S.op("dve", lambda e, o0=o0, src=src: e.tensor_copy(
                                        out=cbo_sb.h[:, o0:o0 + 2], in_=ubuf.h[:, src:src + 2]),
                                         reads=[ubuf.buf], writes=[cbo_sb.buf])
                        elif kind == "sb":
                            o_ = o16[state["o16"] % 3]
                            state["o16"] += 1
                            state["acc"] += 1
                            S.dma("sp", cat_d[idx * 128:(idx + 1) * 128, 0:Tg], o_.h[:], reads=[o_.buf],
                                  writes=[DB("cat", idx)])
                        elif kind == "ca":
                            if g == 1:
                                S.op("dve", lambda e: e.tensor_copy(
                                    out=cbuf.h[:, 0:30], in_=halo_c.h[:, (l * 8 + idx) * 30:(l * 8 + idx) * 30 + 30]),
                                     reads=[halo_c.buf], writes=[cbuf.buf])
                                for s in range(2):
                                    hc = OFF["histc"] + ((l * 2 + s) * 8 + idx) * 30
                                    S.op("dve", lambda e, s=s, hc=hc: e.tensor_copy(
                                        out=cbuf.h[:, 1054 + 62 * s:1084 + 62 * s], in_=par.h[:, hc:hc + 30]),
                                         writes=[cbuf.buf])
                            ac = acc[state["acc"] % 2]
                            state["acc"] += 1
                            n_out = CL - 30
                            w = lambda j: pcol("ccw", (l * 8 + idx) * 31 + j)
                            S.op("dve", lambda e: e.tensor_scalar(
                                out=ac.h[:, 0:n_out], in0=cbuf.h[:, 0:n_out], scalar1=w(0),
                                scalar2=pcol("ccb", l * 8 + idx), op0=ALU.mult, op1=ALU.add),
                                 reads=[cbuf.buf], writes=[ac.buf])
                            for j in range(1, 31):
                                S.op("dve", lambda e, j=j: e.scalar_tensor_tensor(
                                    out=ac.h[:, 0:n_out], in0=cbuf.h[:, j:j + n_out], scalar=w(j),
                                    in1=ac.h[:, 0:n_out], op0=ALU.mult, op1=ALU.add),
                                     reads=[cbuf.buf, ac.buf], writes=[ac.buf])
                            if g == 0:
                                S.op("dve", lambda e: e.tensor_copy(
                                    out=halo_c.h[:, (l * 8 + idx) * 30:(l * 8 + idx) * 30 + 30],
                                    in_=cbuf.h[:, 1024:1054]), reads=[cbuf.buf], writes=[halo_c.buf])
                            else:
                                for s3, src in enumerate((1024, 1086, 1148)):
                                    o0 = (l * 8 + idx) * 90 + s3 * 30
                                    S.op("dve", lambda e, o0=o0, src=src: e.tensor_copy(
                                        out=cco_sb.h[:, o0:o0 + 30], in_=cbuf.h[:, src:src + 30]),
                                         reads=[cbuf.buf], writes=[cco_sb.buf])
                            zr = zcrow[state["zc"] % 2]
                            state["zc"] += 1
                            nC = state["nC"]
                            state["nC"] += 1

                            def later(zr=zr, ac=ac, nC=nC, idx=idx, i=i):
                                segs = [(0, 1024, 0)] + ([(1024, 32, 1054), (1056, 32, 1116)] if g == 1 else [])
                                for (dst, n_, src) in segs:
                                    S.op("act", lambda e, dst=dst, n_=n_, src=src: e.activation(
                                        out=zr.h[:, dst:dst + n_], in_=ac.h[:, src:src + n_], func=AF.Copy),
                                         reads=[ac.buf], writes=[zr.buf])
                                stat_row(S, st, tl1, zr, hl[nC % 2], nC == 0, nC == 7, defer_to=i + 5)
                                S.dma("sp", zc_d[idx * 128:(idx + 1) * 128, 0:Tg], zr.h[:], reads=[zr.buf],
                                      writes=[DB("zc", idx)])

                            S.deferred.append((i + 3, later))

                    run_gemm(S, A.h, A.sub, 32, tl1, 88 if _NB1 is None else _NB1, lambda i, p: w_in[l * 88 + i], ring, pss, epi, epi_end, 2048)
                    for (c, n), stt in zip(tl1, st):
                        S.op("act", lambda e, c=c, n=n, stt=stt: e.activation(
                            out=mean_b.h[:, c:c + n], in_=stt.h[:, 0:n], func=AF.Copy, scale=1.0 / 1024),
                             reads=[stt.buf], writes=[mean_b.buf])
                    S.run()

                with ExitStack() as es:
                    S = Sched(gs)
                    dd = [sb(es, f"ln_d{i}", [128, Tg], F32) for i in range(8)]
                    xr = [sb(es, f"ln_x{i}", [128, Tg], F32) for i in range(2)]
                    sq = [sb(es, f"ln_sq{i}", [128, Tg], F32) for i in range(2)]
                    tmp = sb(es, "ln_tmp", [128, Tg], F32)
                    hl = [(sb(es, f"ln_hi{i}", [128, Tg], BF16), sb(es, f"ln_lo{i}", [128, Tg], BF16)) for i in range(2)]
                    rs = sb(es, "ln_rs", [128, Tg], F32)
                    t1 = [sb(es, f"ln_t{i}", [128, Tg], F32) for i in range(2)]
                    o16 = [sb(es, f"ln_o{i}", [128, Tg], BF16) for i in range(2)]
                    st = [ps(es, f"ln_st{i}", [128, 512]) for i in range(len(tl))]
                    for blk in range(8):
                        x_ = xr[blk % 2]
                        s_ = sq[blk % 2]
                        d_ = dd[blk]
                        S.dma("sp", x_.h[:], zc_d[blk * 128:(blk + 1) * 128, 0:Tg], reads=[DB("zc", blk)],
                              writes=[x_.buf])
                        S.op("dve", lambda e, x_=x_, d_=d_: e.tensor_tensor(out=d_.h[:], in0=x_.h[:],
                                                                           in1=mean_b.h[:, 0:Tg], op=ALU.subtract),
                             reads=[x_.buf, mean_b.buf], writes=[d_.buf])
                        S.op("act", lambda e, s_=s_, d_=d_: e.activation(out=s_.h[:], in_=d_.h[:], func=AF.Square),
                             reads=[d_.buf], writes=[s_.buf])
                        stat_row(S, st, tl, s_, hl[blk % 2], blk == 0, blk == 7)
                    for (c, n), stt in zip(tl, st):
                        S.op("dve", lambda e, c=c, n=n, stt=stt: e.tensor_scalar(
                            out=tmp.h[:, c:c + n], in0=stt.h[:, 0:n], scalar1=1.0 / 1024, scalar2=EPS,
                            op0=ALU.mult, op1=ALU.add), reads=[stt.buf], writes=[tmp.buf])
                    S.op("act", lambda e: e.activation(out=tmp.h[:], in_=tmp.h[:], func=AF.Sqrt),
                         reads=[tmp.buf], writes=[tmp.buf])
                    S.op("dve", lambda e: e.reciprocal(out=rs.h[:], in_=tmp.h[:]), reads=[tmp.buf], writes=[rs.buf])
                    for blk in range(8):
                        d_ = dd[blk]
                        t_ = t1[blk % 2]
                        o_ = o16[blk % 2]
                        S.op("dve", lambda e, d_=d_, t_=t_: e.tensor_tensor(out=t_.h[:], in0=d_.h[:], in1=rs.h[:],
                                                                           op=ALU.mult),
                             reads=[d_.buf, rs.buf], writes=[t_.buf])
                        S.op("dve", lambda e, t_=t_, blk=blk: e.tensor_scalar(
                            out=t_.h[:], in0=t_.h[:], scalar1=pcol("lng", l * 8 + blk), scalar2=pcol("lnb", l * 8 + blk),
                            op0=ALU.mult, op1=ALU.add), reads=[t_.buf], writes=[t_.buf])
                        S.op("act", lambda e, t_=t_, o_=o_: e.activation(out=o_.h[:], in_=t_.h[:], func=AF.Silu),
                             reads=[t_.buf], writes=[o_.buf])
                        S.dma("sp", cat_d[(8 + blk) * 128:(9 + blk) * 128, 0:Tg], o_.h[:], reads=[o_.buf],
                              writes=[DB("cat", 8 + blk)])
                    S.run()

                with ExitStack() as es2:
                    A = sb(es2, "g2_A", [128, 32, Tg], BF16, nsub=32)
                    with ExitStack() as es:
                        S = Sched(gs)
                        KW = 1024 if g == 0 else 1600
                        C0 = 0 if g == 0 else 512
                        NT = KW // 128 if g == 0 else 12
                        qh = [sb(es, f"at_q{i}", [128, Tg], BF16) for i in range(2)]
                        kh = [sb(es, f"at_k{i}", [128, KW], BF16) for i in range(2)]
                        vh = [sb(es, f"at_v{i}", [128, KW], BF16) for i in range(2)]
                        vt = [sb(es, f"at_vt{i}", [128, NT, 130], BF16) for i in range(2)]
                        yt = [sb(es, f"at_yt{i}", [128, 128], BF16) for i in range(2)]
                        bp = [sb(es, f"at_bp{i}", [128, 640], F32) for i in range(2)]
                        ssb = [sb(es, f"at_s{i}", [128, 640], F32) for i in range(2)]
                        eb = [sb(es, f"at_e{i}", [128, 640], BF16) for i in range(2)]
                        rd = [sb(es, f"at_rd{i}", [128, 128], F32) for i in range(2)]
                        sps = [ps(es, f"at_sps{i}", [128, 1024]) for i in range(2)]
                        ods = [ps(es, f"at_od{i}", [128, 512]) for i in range(2)]
                        trp = [ps(es, f"at_tr{i}", [128, 1024], BF16) for i in range(2)]
                        if g == 1:
                            kcs = [[sb(es, f"at_kc{i}{s}", [128, 512], BF16) for s in range(2)] for i in range(2)]
                            vcs = [[sb(es, f"at_vc{i}{s}", [128, 4, 130], BF16) for s in range(2)] for i in range(2)]
                            bs = [sb(es, f"at_bs{i}", [128, 160], F32) for i in range(2)]
                            vn = [[sb(es, f"at_vn{i}{s}", [32, 130], BF16) for s in range(2)] for i in range(2)]
                        cnt = {"sp": 0, "od": 0, "tr": 0, "x": 0}
                        for i_ in range(2):
                            S.op("dve", lambda e, i_=i_: e.memset(vt[i_].h[:, :, 128:130], 1.0), writes=[vt[i_].buf])
                            if g == 1:
                                for s_ in range(2):
                                    S.op("dve", lambda e, i_=i_, s_=s_: e.memset(vcs[i_][s_].h[:, :, 128:130], 1.0),
                                         writes=[vcs[i_][s_].buf])
                                    S.op("dve", lambda e, i_=i_, s_=s_: e.memset(vn[i_][s_].h[:, 128:130], 1.0),
                                         writes=[vn[i_][s_].buf])

                        def load_head(h):
                            p = h % 2
                            r0 = (l * 16 + h) * 128
                            S.dma("sp", qh[p].h[:], q_d[h * 128:(h + 1) * 128, 0:Tg], reads=[DB("q", h)],
                                  writes=[qh[p].buf])
                            S.dma("sp", kh[p].h[:], k_d[r0:r0 + 128, C0:C0 + KW], reads=[DB("k", l, h)],
                                  writes=[kh[p].buf])
                            S.dma("sp", vh[p].h[:], v_d[r0:r0 + 128, C0:C0 + KW], reads=[DB("v", l, h)],
                                  writes=[vh[p].buf])
                            S.dma("sp", bp[p].h[:], biasp_d[l * 16 + h], writes=[bp[p].buf])
                            if g == 1:
                                S.dma("sp", bs[p].h[:], biass_d[l * 16 + h], writes=[bs[p].buf])
                                for s in range(2):
                                    S.dma("pool", kcs[p][s].h[:], ckT_d[(l * 2 + s) * 16 + h], writes=[kcs[p][s].buf])
                                    S.dma("pool", vcs[p][s].h[:, :, 0:128],
                                          cv_d[l * 2 + s][:, h, :].rearrange("(j k) d -> k j d", k=128),
                                          writes=[vcs[p][s].buf])

                        load_head(0)
                        for h in range(16):
                            p = h % 2
                            if h + 1 < 16:
                                load_head(h + 1)
                            for t0 in range(0, NT, 4):
                                nt = min(4, NT - t0)
                                tp = trp[cnt["tr"] % 2]
                                cnt["tr"] += 1

                                def emit(e, t0=t0, nt=nt, tp=tp, p=p):
                                    for t in range(nt):
                                        ins = e.transpose(tp.h[:, t * 128:(t + 1) * 128],
                                                          vh[p].h[:, (t0 + t) * 128:(t0 + t + 1) * 128], ident_b.h[:])
                                    return ins

                                S.op("pe", emit, reads=[vh[p].buf, ident_b.buf], writes=[tp.buf])
                                S.op("act", lambda e, t0=t0, nt=nt, tp=tp, p=p: e.activation(
                                    out=vt[p].h[:, t0:t0 + nt, 0:128],
                                    in_=tp.h[:, 0:nt * 128].rearrange("p (t d) -> p t d", d=128), func=AF.Copy),
                                     reads=[tp.buf], writes=[vt[p].buf])
                            if g == 1:
                                for s in range(2):
                                    tp = trp[cnt["tr"] % 2]
                                    cnt["tr"] += 1
                                    S.op("pe", lambda e, tp=tp, s=s, p=p: e.transpose(
                                        tp.h[0:32, 0:128], vh[p].h[:, 1536 + 32 * s:1568 + 32 * s], ident_b.h[:]),
                                         reads=[vh[p].buf, ident_b.buf], writes=[tp.buf])
                                    S.op("act", lambda e, tp=tp, s=s, p=p: e.activation(
                                        out=vn[p][s].h[:, 0:128], in_=tp.h[0:32, 0:128], func=AF.Copy),
                                         reads=[tp.buf], writes=[vn[p][s].buf])
                            jobs = []
                            for ml in range(8):
                                m = 8 * g + ml
                                j0 = max(0, 4 - m)
                                x = cnt["x"] % 2
                                cnt["x"] += 1
                                sp_, od = sps[x], ods[x]
                                qa = qh[p].h[:, ml * 128:(ml + 1) * 128]

                                def stA_(sp_=sp_, x=x, j0=j0, m=m, qa=qa, p=p):
                                    def emit_s(e):
                                        for j in range(j0, 5):
                                            kc0 = 128 * (m - 4 + j) - C0
                                            ins = e.matmul(sp_.h[:, j * 128:(j + 1) * 128],
                                                           lhsT=kh[p].h[:, kc0:kc0 + 128], rhs=qa, start=True, stop=True)
                                        return ins

                                    S.op("pe", emit_s, reads=[kh[p].buf, qh[p].buf], writes=[sp_.buf])
                                    a0, a1 = j0 * 128, 640
                                    S.op("dve", lambda e: e.scalar_tensor_tensor(
                                        out=ssb[x].h[:, a0:a1], in0=sp_.h[:, a0:a1], scalar=SCALE,
                                        in1=bp[p].h[:, a0:a1], op0=ALU.mult, op1=ALU.add),
                                         reads=[sp_.buf, bp[p].buf], writes=[ssb[x].buf])
                                    S.op("act", lambda e: e.activation(out=eb[x].h[:, a0:a1], in_=ssb[x].h[:, a0:a1],
                                                                       func=AF.Exp),
                                         reads=[ssb[x].buf], writes=[eb[x].buf])

                                def stB_(od=od, x=x, j0=j0, m=m, p=p):
                                    def emit_o(e):
                                        for j in range(j0, 5):
                                            ti = m - 4 + j - C0 // 128
                                            ins = e.matmul(od.h[:, 0:130], lhsT=eb[x].h[:, j * 128:(j + 1) * 128],
                                                           rhs=vt[p].h[:, ti, :], start=(j == j0), stop=(j == 4))
                                        return ins

                                    S.op("pe", emit_o, reads=[vt[p].buf, eb[x].buf], writes=[od.buf])
                                    S.op("dve", lambda e: e.reciprocal(out=rd[x].h[:, 0:1], in_=od.h[:, 128:129]),
                                         reads=[od.buf], writes=[rd[x].buf])
                                    S.op("act", lambda e: e.activation(out=yt[x].h[:], in_=od.h[:, 0:128], func=AF.Copy,
                                                                       scale=rd[x].h[:, 0:1]),
                                         reads=[od.buf, rd[x].buf], writes=[yt[x].buf])

                                def stC_(x=x, h=h, ml=ml):
                                    tp = trp[cnt["tr"] % 2]
                                    cnt["tr"] += 1
                                    S.op("pe", lambda e: e.transpose(tp.h[:, 0:128], yt[x].h[:], ident_b.h[:]),
                                         reads=[yt[x].buf, ident_b.buf], writes=[tp.buf])
                                    S.op("dve", lambda e: e.tensor_copy(out=A.h[:, h, ml * 128:(ml + 1) * 128],
                                                                        in_=tp.h[:, 0:128]),
                                         reads=[tp.buf], writes=[A.sub[h]])

                                jobs.append((stA_, stB_, stC_))
                            if g == 1:
                                for s_i in range(2):
                                    x = cnt["x"] % 2
                                    cnt["x"] += 1
                                    sp_, od = sps[x], ods[x]
                                    qa = qh[p].h[:, 1024 + 32 * s_i:1056 + 32 * s_i]
                                    kn = kh[p].h[:, 1536 + 32 * s_i:1568 + 32 * s_i]

                                    def stA_(sp_=sp_, x=x, qa=qa, kn=kn, p=p, s=s_i):
                                        def emit_s(e):
                                            for jt in range(4):
                                                e.matmul(sp_.h[:, jt * 32:(jt + 1) * 32],
                                                         lhsT=kcs[p][s].h[:, jt * 128:(jt + 1) * 128], rhs=qa,
                                                         start=True, stop=True)
                                            return e.matmul(sp_.h[0:32, 128:160], lhsT=kn, rhs=qa, start=True, stop=True)

                                        S.op("pe", emit_s, reads=[kcs[p][s].buf, kh[p].buf, qh[p].buf], writes=[sp_.buf])
                                        for (r, a0, a1) in ((128, 0, 128), (32, 128, 160)):
                                            S.op("dve", lambda e, r=r, a0=a0, a1=a1: e.scalar_tensor_tensor(
                                                out=ssb[x].h[0:r, a0:a1], in0=sp_.h[0:r, a0:a1], scalar=SCALE,
                                                in1=bs[p].h[0:r, a0:a1], op0=ALU.mult, op1=ALU.add),
                                                 reads=[sp_.buf, bs[p].buf], writes=[ssb[x].buf])
                                            S.op("act", lambda e, r=r, a0=a0, a1=a1: e.activation(
                                                out=eb[x].h[0:r, a0:a1], in_=ssb[x].h[0:r, a0:a1], func=AF.Exp),
                                                 reads=[ssb[x].buf], writes=[eb[x].buf])

                                    def stB_(od=od, x=x, p=p, s=s_i):
                                        def emit_o(e):
                                            for jt in range(4):
                                                e.matmul(od.h[0:32, 0:130], lhsT=eb[x].h[:, jt * 32:(jt + 1) * 32],
                                                         rhs=vcs[p][s].h[:, jt, :], start=(jt == 0), stop=False)
                                            return e.matmul(od.h[0:32, 0:130], lhsT=eb[x].h[0:32, 128:160],
                                                            rhs=vn[p][s].h[0:32, :], start=False, stop=True)

                                        S.op("pe", emit_o, reads=[vcs[p][s].buf, vn[p][s].buf, eb[x].buf], writes=[od.buf])
                                        S.op("dve", lambda e: e.reciprocal(out=rd[x].h[0:32, 0:1], in_=od.h[0:32, 128:129]),
                                             reads=[od.buf], writes=[rd[x].buf])
                                        S.op("act", lambda e: e.activation(out=yt[x].h[0:32, :], in_=od.h[0:32, 0:128],
                                                                           func=AF.Copy, scale=rd[x].h[0:32, 0:1]),
                                             reads=[od.buf, rd[x].buf], writes=[yt[x].buf])

                                    def stC_(x=x, h=h, s=s_i):
                                        tp = trp[cnt["tr"] % 2]
                                        cnt["tr"] += 1
                                        S.op("pe", lambda e: e.transpose(tp.h[:, 0:32], yt[x].h[0:32, :], ident_b.h[0:32, 0:32]),
                                             reads=[yt[x].buf, ident_b.buf], writes=[tp.buf])
                                        S.op("dve", lambda e: e.tensor_copy(out=A.h[:, h, 1024 + 32 * s:1056 + 32 * s],
                                                                            in_=tp.h[:, 0:32]),
                                             reads=[tp.buf], writes=[A.sub[h]])

                                    jobs.append((stA_, stB_, stC_))
                            jobs[0][0]()
                            for t in range(len(jobs)):
                                if t + 1 < len(jobs):
                                    jobs[t + 1][0]()
                                jobs[t][1]()
                                if t >= 1:
                                    jobs[t - 1][2]()
                            jobs[-1][2]()
                        S.run()

                    with ExitStack() as es:
                        S = Sched(gs)
                        ring = [sb(es, f"g2_w{i}", [128, 4096], BF16) for i in range(6)]
                        xr = [sb(es, f"g2_xr{i}", [128, Tg], F32) for i in range(2)]
                        orow = [sb(es, f"g2_or{i}", [128, Tg], F32) for i in range(2)]
                        sq = [sb(es, f"g2_sq{i}", [128, Tg], F32) for i in range(2)]
                        tmp = sb(es, "g2_tmp", [128, Tg], F32)
                        hl = [(sb(es, f"g2_hi{i}", [128, Tg], BF16), sb(es, f"g2_lo{i}", [128, Tg], BF16)) for i in range(2)]
                        pss = [ps(es, f"g2_ps{i}", [128, 512]) for i in range(4)]
                        tle = tiles_eq(Tg)
                        st = [ps(es, f"g2_st{i}", [128, 512]) for i in range(len(tle))]
                        cat3 = cat_d.rearrange("(k p) t -> p k t", p=128)
                        for hf in range(4):
                            S.dma("sp", A.h[:, 16 + 4 * hf:20 + 4 * hf, :], cat3[:, 4 * hf:4 * hf + 4, 0:Tg],
                                  reads=[DB("cat", i) for i in range(4 * hf, 4 * hf + 4)],
                                  writes=[A.sub[k] for k in range(16 + 4 * hf, 20 + 4 * hf)])

                        def epi(i, ti, c, n, pt):
                            if ti == 0:
                                x_ = xr[i % 2]
                                S.dma("sp", x_.h[:], x_src[i * 128:(i + 1) * 128, G0:G0 + Tg], reads=[DB("x", i)],
                                      writes=[x_.buf])
                            S.op("dve", lambda e: e.tensor_tensor(out=orow[i % 2].h[:, c:c + n], in0=pt.h[:, 0:n],
                                                                  in1=xr[i % 2].h[:, c:c + n], op=ALU.add),
                                 reads=[pt.buf, xr[i % 2].buf], writes=[orow[i % 2].buf])

                        def epi_end(i):
                            o_ = orow[i % 2]
                            s_ = sq[i % 2]
                            S.dma("sp", x_d[i * 128:(i + 1) * 128, G0:G0 + Tg], o_.h[:], reads=[o_.buf],
                                  writes=[DB("x", i)])
                            S.op("act", lambda e: e.activation(out=s_.h[:], in_=o_.h[:], func=AF.Square),
                                 reads=[o_.buf], writes=[s_.buf])
                            stat_row(S, st, tle, s_, hl[i % 2], i == 0, i == 31, defer_to=i + 1)

                        run_gemm(S, A.h, A.sub, 32, tle, 32, lambda i, p: w_out[l * 32 + i], ring, pss, epi, epi_end, 2048)
                        rstd_from_stats(S, st, tle, 0, tmp)
                        S.run()

                with ExitStack() as es:
                    S = Sched(gs)
                    A = sb(es, "g3_A", [128, 32, Tg], BF16, nsub=32)
                    ring = [sb(es, f"g3_w{i}", [128, 4096], BF16) for i in range(6)]
                    xr = [sb(es, f"g3_xr{i}", [128, Tg], F32) for i in range(4)]
                    stA = [sb(es, f"g3_st{i}", [128, Tg], F32) for i in range(2)]
                    o16 = [sb(es, f"g3_o{i}", [128, Tg], BF16) for i in range(3)]
                    pss = [ps(es, f"g3_ps{i}", [128, 512]) for i in range(6)]
                    for blk in range(32):
                        x_ = xr[blk % 4]
                        S.dma("sp" if blk % 2 == 0 else "act", x_.h[:], x_d[blk * 128:(blk + 1) * 128, G0:G0 + Tg],
                              reads=[DB("x", blk)], writes=[x_.buf])
                        S.op("dve", lambda e, x_=x_, blk=blk: e.scalar_tensor_tensor(
                            out=A.h[:, blk, :], in0=x_.h[:], scalar=pcol("gffn", l * 32 + blk),
                            in1=rstd_b.h[:, 0:Tg], op0=ALU.mult, op1=ALU.mult),
                             reads=[x_.buf, rstd_b.buf], writes=[A.sub[blk]])

                    def epi(i, ti, c, n, pt):
                        f = i // 2
                        sa = stA[f % 2]
                        if i % 2 == 0:
                            S.op("act", lambda e: e.activation(out=sa.h[:, c:c + n], in_=pt.h[:, 0:n], func=AF.Silu),
                                 reads=[pt.buf], writes=[sa.buf])
                        else:
                            o_ = o16[f % 3]
                            S.op("dve", lambda e: e.tensor_tensor(out=o_.h[:, c:c + n], in0=pt.h[:, 0:n],
                                                                  in1=sa.h[:, c:c + n], op=ALU.mult),
                                 reads=[pt.buf, sa.buf], writes=[o_.buf])

                    def epi_end(i):
                        if i % 2 == 1:
                            f = i // 2
                            o_ = o16[f % 3]
                            for sg_ in range(2):
                                cl_ = 512 * sg_
                                ts_ = 512 if sg_ == 0 else Tg - 512
                                S.dma("sp", hid_d[sg_ * 128:(sg_ + 1) * 128, f * ts_:(f + 1) * ts_], o_.h[:, cl_:cl_ + ts_],
                                      reads=[o_.buf], writes=[DB("hid", sg_, f)])

                    run_gemm(S, A.h, A.sub, 32, tiles_eq(Tg), 172, lambda i, p: w_gu[l * 172 + i], ring, pss, epi, epi_end, 2048)
                    S.run()

                for sg in range(2):
                    c_lo = 512 * sg
                    Ts = 512 if sg == 0 else Tg - 512
                    tls = tiles_eq(Ts)
                    with ExitStack() as es:
                        S = Sched(gs)
                        A = sb(es, "g4_A", [128, FC, Ts], BF16, nsub=22)
                        ring = [sb(es, f"g4_w{i}", [128, 5504], BF16) for i in range(5)]
                        xr = [sb(es, f"g4_xr{i}", [128, Ts], F32) for i in range(2)]
                        orow = [sb(es, f"g4_or{i}", [128, Ts], F32) for i in range(2)]
                        sq = [sb(es, f"g4_sq{i}", [128, Ts], F32) for i in range(2)]
                        tmp = sb(es, "g4_tmp", [128, Ts], F32)
                        hl = [(sb(es, f"g4_hi{i}", [128, Ts], BF16), sb(es, f"g4_lo{i}", [128, Ts], BF16)) for i in range(2)]
                        pss = [ps(es, f"g4_ps{i}", [128, 512]) for i in range(4)]
                        st = [ps(es, f"g4_st{i}", [128, 512]) for i in range(len(tls))]
                        for hf in range(22):
                            k0, k1 = 4 * hf, min(FC, 4 * hf + 4)
                            S.dma("sp" if hf % 2 == 0 else "act", A.h[:, k0:k1, :],
                                  hid_d[sg * 128:(sg + 1) * 128, k0 * Ts:k1 * Ts].rearrange("p (k t) -> p k t", t=Ts),
                                  reads=[DB("hid", sg, f) for f in range(k0, k1)], writes=[A.sub[hf]])

                        def epi(i, ti, c, n, pt):
                            if ti == 0:
                                x_ = xr[i % 2]
                                S.dma("sp", x_.h[:], x_d[i * 128:(i + 1) * 128, G0 + c_lo:G0 + c_lo + Ts],
                                      reads=[DB("x", i)], writes=[x_.buf])
                            S.op("dve", lambda e: e.tensor_tensor(out=orow[i % 2].h[:, c:c + n], in0=pt.h[:, 0:n],
                                                                  in1=xr[i % 2].h[:, c:c + n], op=ALU.add),
                                 reads=[pt.buf, xr[i % 2].buf], writes=[orow[i % 2].buf])

                        def epi_end(i):
                            o_ = orow[i % 2]
                            s_ = sq[i % 2]
                            S.dma("sp", x_d[i * 128:(i + 1) * 128, G0 + c_lo:G0 + c_lo + Ts], o_.h[:], reads=[o_.buf],
                                  writes=[DB("x", i)])
                            S.op("act", lambda e: e.activation(out=s_.h[:], in_=o_.h[:], func=AF.Square),
                                 reads=[o_.buf], writes=[s_.buf])
                            stat_row(S, st, tls, s_, hl[i % 2], i == 0, i == 31, defer_to=i + 1)

                        run_gemm(S, A.h, A.sub, FC, tls, 32, lambda i, p: w_dn[l * 32 + i][:, p * 5504:(p + 1) * 5504], ring, pss, epi, epi_end, 1376, P=2)
                        rstd_from_stats(S, st, tls, c_lo, tmp)
                        S.run()

            with ExitStack() as es:
                S = Sched(gs)
                xr = [sb(es, f"fn_x{i}", [128, Tg], F32) for i in range(3)]
                orow = [sb(es, f"fn_o{i}", [128, Tg], F32) for i in range(3)]
                for blk in range(32):
                    x_ = xr[blk % 3]
                    o_ = orow[blk % 3]
                    S.dma("sp", x_.h[:], x_d[blk * 128:(blk + 1) * 128, G0:G0 + Tg], reads=[DB("x", blk)],
                          writes=[x_.buf])
                    S.op("dve", lambda e, x_=x_, o_=o_, blk=blk: e.scalar_tensor_tensor(
                        out=o_.h[:], in0=x_.h[:], scalar=pcol("gfin", blk), in1=rstd_b.h[:, 0:Tg],
                        op0=ALU.mult, op1=ALU.mult), reads=[x_.buf, rstd_b.buf], writes=[o_.buf])
                    S.dma("sp", yT[blk * 128:(blk + 1) * 128, G0:G0 + Tg], o_.h[:], reads=[o_.buf])
                if g == 1:
                    S.dma("sp", cbo_d, cbo_sb.h[:], reads=[cbo_sb.buf])
                    S.dma("sp", cco_d, cco_sb.h[:], reads=[cco_sb.buf])
                S.run()
    return nc


def _blockify(w, perm=None):
    K, N = w.shape
    a = w.reshape(K // 128, 128, N // 128, 128).transpose(2, 1, 0, 3)
    if perm is not None:
        a = a[perm]
    return np.ascontiguousarray(a).reshape(a.shape[0], 128, K)


def _pvec(v):
    return np.ascontiguousarray(v.reshape(-1, 128).T)


_CACHE = {}


def _prep(x_prompt, x_sample, cache_attn_k, cache_attn_v, cache_conv_b, cache_conv_c,
          norm_mix_g, w_in, rel_bias, conv_b_w, conv_c_w, conv_c_b, ln_c_g, ln_c_b,
          w_out, norm_ffn_g, w_ffn_gate, w_ffn_up, w_ffn_down, final_norm_g, cores=range(8)):
    f = lambda a: np.asarray(a, dtype=np.float32)
    x_prompt, x_sample = f(x_prompt), f(x_sample)
    cache_attn_k, cache_attn_v = f(cache_attn_k), f(cache_attn_v)
    cache_conv_b, cache_conv_c = f(cache_conv_b), f(cache_conv_c)
    rel_bias = f(rel_bias)
    n_cores = 8

    perm = [ORIG_BLK[k] + i for (k, i) in IN_ORDER]
    w_in_b = np.concatenate([_blockify(f(w_in[l]), perm) for l in range(L)], axis=0)
    w_out_b = np.concatenate([_blockify(f(w_out[l])) for l in range(L)], axis=0)
    gu = []
    for l in range(L):
        gb = _blockify(f(w_ffn_gate[l]))
        ub = _blockify(f(w_ffn_up[l]))
        gu.append(np.stack([gb, ub], axis=1).reshape(172, 128, 4096))
    w_gu_b = np.concatenate(gu, axis=0)
    w_dn_b = np.concatenate([_blockify(f(w_ffn_down[l])) for l in range(L)], axis=0)

    k = np.arange(128)[:, None, None]
    j = np.arange(5)[None, :, None]
    q = np.arange(128)[None, None, :]
    d = q - k + 128 * (4 - j)
    idx = np.clip(d, -256, 256) + 256
    masked = ((j == 4) & (k >= 64) & (q < 64)) | ((j == 0) & (k < 64) & (q >= 64))
    biasp = np.where(masked[None, None], np.float32(NEG), rel_bias[:, :, idx]).astype(np.float32)
    biasp = np.ascontiguousarray(biasp.reshape(L * 16, 128, 640))
    q32 = np.arange(32)[None, None, :]
    ds = np.where(j < 4, q32 + 512 - (128 * j + k), q32 - k)
    idxs = np.clip(ds, -256, 256) + 256
    biass = np.ascontiguousarray(rel_bias[:, :, idxs].astype(np.float32).reshape(L * 16, 128, 160))
    ident = np.eye(128, dtype=np.float32)

    def par_common():
        p = np.zeros((128, NPAR), np.float32)
        for l in range(L):
            p[:, OFF["gmix"] + l * 32: OFF["gmix"] + (l + 1) * 32] = _pvec(f(norm_mix_g[l]))
            p[:, OFF["gffn"] + l * 32: OFF["gffn"] + (l + 1) * 32] = _pvec(f(norm_ffn_g[l]))
            cb = f(conv_b_w[l]).reshape(3, 8, 128).transpose(2, 1, 0).reshape(128, 24)
            p[:, OFF["cbw"] + l * 24: OFF["cbw"] + (l + 1) * 24] = cb
            cc = f(conv_c_w[l]).reshape(31, 8, 128).transpose(2, 1, 0).reshape(128, 248)
            p[:, OFF["ccw"] + l * 248: OFF["ccw"] + (l + 1) * 248] = cc
            p[:, OFF["ccb"] + l * 8: OFF["ccb"] + (l + 1) * 8] = _pvec(f(conv_c_b[l]))
            p[:, OFF["lng"] + l * 8: OFF["lng"] + (l + 1) * 8] = _pvec(f(ln_c_g[l]))
            p[:, OFF["lnb"] + l * 8: OFF["lnb"] + (l + 1) * 8] = _pvec(f(ln_c_b[l]))
        p[:, OFF["gfin"]: OFF["gfin"] + 32] = _pvec(f(final_norm_g))
        return p

    pc = par_common()
    in_maps = []
    for c in cores:
        xTc = np.ascontiguousarray(np.concatenate(
            [x_prompt[c].T, x_sample[2 * c].T, x_sample[2 * c + 1].T], axis=1))
        p = pc.copy()
        hb = cache_conv_b[:, 2 * c:2 * c + 2].reshape(L, 2, 2, 8, 128).transpose(4, 0, 1, 3, 2).reshape(128, -1)
        hc = cache_conv_c[:, 2 * c:2 * c + 2].reshape(L, 2, 30, 8, 128).transpose(4, 0, 1, 3, 2).reshape(128, -1)
        p[:, OFF["histb"]:OFF["histb"] + hb.shape[1]] = hb
        p[:, OFF["histc"]:OFF["histc"] + hc.shape[1]] = hc
        ck = cache_attn_k[:, 2 * c:2 * c + 2]
        ckT = np.ascontiguousarray(ck.transpose(0, 1, 3, 4, 2)).reshape(L * 2 * 16, 128, 512)
        cv = np.ascontiguousarray(cache_attn_v[:, 2 * c:2 * c + 2]).reshape(L * 2, 512, 16, 128)
        in_maps.append({"xT": xTc, "w_in": w_in_b, "w_out": w_out_b, "w_gu": w_gu_b, "w_dn": w_dn_b, "par": p,
                        "biasp": biasp, "biass": biass, "ckT": ckT, "cv": cv, "ident": ident})

    return in_maps


def kernel(**inputs):
    n_cores = 8
    in_maps = _prep(**inputs)
    if "nc" not in _CACHE:
        _CACHE["nc"] = build_program()
    nc = _CACHE["nc"]
    res = run_bass_kernel_spmd(nc, in_maps, core_ids=list(range(n_cores)))
    R = res.results

    y_prompt = np.empty((8, 2048, D), np.float32)
    y_sample = np.empty((16, 32, D), np.float32)
    pk = np.empty((L, 8, 512, 16, 128), np.float32)
    pv = np.empty_like(pk)
    sk = np.empty((L, 16, 32, 16, 128), np.float32)
    sv = np.empty_like(sk)
    pb = np.empty((L, 8, 2, 1024), np.float32)
    pcv = np.empty((L, 8, 30, 1024), np.float32)
    sbo = np.empty((L, 16, 2, 1024), np.float32)
    sco = np.empty((L, 16, 30, 1024), np.float32)
    for c in range(n_cores):
        r = R[c]
        yT = r["yT"]
        y_prompt[c] = yT[:, :2048].T
        for s in range(2):
            y_sample[2 * c + s] = yT[:, 2048 + 32 * s:2080 + 32 * s].T
        for name, P_, S_ in (("kTo", pk, sk), ("vTo", pv, sv)):
            a = r[name].reshape(L, 2048, 576)
            for l in range(L):
                P_[l, c] = a[l][:, :512].T.reshape(512, 16, 128)
                for s in range(2):
                    S_[l, 2 * c + s] = a[l][:, 512 + 32 * s:544 + 32 * s].T.reshape(32, 16, 128)
        cb = r["cbo"].reshape(128, L, 8, 3, 2)
        cc = r["cco"].reshape(128, L, 8, 3, 30)
        for l in range(L):
            pb[l, c] = cb[:, l, :, 0, :].transpose(2, 1, 0).reshape(2, 1024)
            pcv[l, c] = cc[:, l, :, 0, :].transpose(2, 1, 0).reshape(30, 1024)
            for s in range(2):
                sbo[l, 2 * c + s] = cb[:, l, :, 1 + s, :].transpose(2, 1, 0).reshape(2, 1024)
                sco[l, 2 * c + s] = cc[:, l, :, 1 + s, :].transpose(2, 1, 0).reshape(30, 1024)
    return (y_prompt, y_sample, pk, pv, pb, pcv, sk, sv, sbo, sco)
```
